# Optimizing a Trainium2 kernel written in Bass

```python
import jax, jax.numpy as jnp
from jax import lax
import numpy as np

D_MODEL = 1024
BATCH = 4
SEQ = 8192
DEPTH = 4

CHUNK = 64
D_MIX = D_MODEL
D_A = D_MIX // 2
D_B = D_MIX - D_A
A_HEADS = 4
A_BLOCK = 128
A_HEAD_DIM = D_A // A_HEADS
B_HEADS = 4
B_DK = D_B // 2
B_DV = D_B
B_HEAD_K = B_DK // B_HEADS
B_HEAD_V = B_DV // B_HEADS
B_GATE_RANK = 16
B_GATE_TAU = 16.0
D_FF = 4 * D_MODEL
D_IN = 2 * D_A + 2 * B_DK + 2 * B_DV + B_GATE_RANK
EPS = 1e-6

kernel_name = "hybrid_gmlp_gla_sqrelu_sandwich"


def rmsnorm(x, g):
    xf = x.astype(jnp.float32)
    xf = xf * lax.rsqrt(jnp.mean(xf * xf, axis=-1, keepdims=True) + EPS)
    return (xf * g.astype(jnp.float32)).astype(x.dtype)


def layernorm(x, g, b):
    xf = x.astype(jnp.float32)
    mu = jnp.mean(xf, axis=-1, keepdims=True)
    xc = xf - mu
    xf = xc * lax.rsqrt(jnp.mean(xc * xc, axis=-1, keepdims=True) + EPS)
    return (xf * g.astype(jnp.float32) + b.astype(jnp.float32)).astype(x.dtype)


def gmlp_mixer(zu, zv, ln_g, ln_b, ws, bs, out_g):
    bsz, s, _ = zu.shape
    u = jax.nn.gelu(zu)
    v = layernorm(jax.nn.gelu(zv), ln_g, ln_b)
    n_blk = s // A_BLOCK
    v = v.reshape(bsz, n_blk, A_BLOCK, A_HEADS, A_HEAD_DIM)
    pos = jnp.arange(A_BLOCK) // CHUNK
    mask = pos[:, None] >= pos[None, :]
    w = jnp.where(mask[None], ws, jnp.zeros_like(ws))
    mixed = jnp.einsum('hij,bnjhc->bnihc', w, v) + jnp.transpose(bs)[None, None, :, :, None]
    out = u * mixed.reshape(bsz, s, D_A)
    return rmsnorm(out, out_g)


def gla_mixer(zq, zk, zv, zg, zlr, gate_w, gate_b, out_g):
    dt = zq.dtype
    bsz, s, _ = zq.shape
    n_c = s // CHUNK
    f32 = jnp.float32
    q = zq.astype(f32) * (B_HEAD_K ** -0.5)
    k = zk.astype(f32)
    v = zv.astype(f32)
    log_a = jax.nn.log_sigmoid(zlr.astype(f32) @ gate_w.astype(f32) + gate_b.astype(f32)) / B_GATE_TAU

    def to_chunks(t, d):
        return jnp.transpose(t.reshape(bsz, n_c, CHUNK, B_HEADS, d), (0, 3, 1, 2, 4))

    q, k, log_a = to_chunks(q, B_HEAD_K), to_chunks(k, B_HEAD_K), to_chunks(log_a, B_HEAD_K)
    v = to_chunks(v, B_HEAD_V)
    cum = jnp.cumsum(log_a, axis=-2)
    last = cum[..., -1:, :]
    qe = q * jnp.exp(cum)
    ke = k * jnp.exp(-cum)
    kd = k * jnp.exp(last - cum)

    tril = jnp.tril(jnp.ones((CHUNK, CHUNK), dtype=bool))
    scores = jnp.einsum('bhnid,bhnjd->bhnij', qe, ke)
    scores = jnp.where(tril, scores, 0.0)
    o_intra = jnp.einsum('bhnij,bhnjv->bhniv', scores, v)

    upd = jnp.einsum('bhnjd,bhnjv->bhndv', kd, v)
    dec = jnp.exp(last[..., 0, :])

    def step(state, inp):
        d_c, u_c = inp
        return d_c[..., None] * state + u_c, state

    s0 = jnp.zeros((bsz, B_HEADS, B_HEAD_K, B_HEAD_V), f32)
    _, s_prev = lax.scan(step, s0, (jnp.moveaxis(dec, 2, 0), jnp.moveaxis(upd, 2, 0)))
    s_prev = jnp.moveaxis(s_prev, 0, 2)
    o_inter = jnp.einsum('bhnid,bhndv->bhniv', qe, s_prev)

    o = jnp.transpose(o_intra + o_inter, (0, 2, 3, 1, 4))
    o = o.reshape(bsz, s, B_HEADS, B_HEAD_V)
    o = rmsnorm(o, out_g.reshape(B_HEADS, B_HEAD_V)).reshape(bsz, s, B_DV)
    return (o * jax.nn.silu(zg.astype(f32))).astype(dt)


def setup_inputs(seed: int = 0) -> dict:
    key = jax.random.key(seed)
    ks = jax.random.split(key, 20)
    n = jax.random.normal

    def gain(k, d):
        return 1.0 + 0.05 * n(k, (DEPTH, d), jnp.float32)

    return {
        "x": n(ks[0], (BATCH, SEQ, D_MODEL), jnp.float32),
        "pre_mix_g": gain(ks[1], D_MODEL),
        "w_in": n(ks[2], (DEPTH, D_MODEL, D_IN), jnp.float32) * D_MODEL ** -0.5,
        "gmlp_ln_g": gain(ks[3], D_A),
        "gmlp_ln_b": 0.02 * n(ks[4], (DEPTH, D_A), jnp.float32),
        "gmlp_ws": n(ks[5], (DEPTH, A_HEADS, A_BLOCK, A_BLOCK), jnp.float32) * A_BLOCK ** -0.5,
        "gmlp_bs": 1.0 + 0.1 * n(ks[6], (DEPTH, A_HEADS, A_BLOCK), jnp.float32),
        "gmlp_out_g": gain(ks[7], D_A),
        "gla_gate_w": n(ks[8], (DEPTH, B_GATE_RANK, B_DK), jnp.float32) * B_GATE_RANK ** -0.5,
        "gla_gate_b": 0.5 * n(ks[9], (DEPTH, B_DK), jnp.float32),
        "gla_out_g": gain(ks[10], B_DV),
        "w_out": n(ks[11], (DEPTH, D_MIX, D_MODEL), jnp.float32) * D_MIX ** -0.5,
        "post_mix_g": gain(ks[12], D_MODEL),
        "pre_ff_g": gain(ks[13], D_MODEL),
        "w_ff1": n(ks[14], (DEPTH, D_MODEL, D_FF), jnp.float32) * D_MODEL ** -0.5,
        "w_ff2": n(ks[15], (DEPTH, D_FF, D_MODEL), jnp.float32) * D_FF ** -0.5,
        "post_ff_g": gain(ks[16], D_MODEL),
    }


def reference(x, pre_mix_g, w_in, gmlp_ln_g, gmlp_ln_b, gmlp_ws, gmlp_bs, gmlp_out_g,
              gla_gate_w, gla_gate_b, gla_out_g, w_out, post_mix_g, pre_ff_g,
              w_ff1, w_ff2, post_ff_g):
    cuts = [D_A, 2 * D_A, 2 * D_A + B_DK, 2 * D_A + 2 * B_DK,
            2 * D_A + 2 * B_DK + B_DV, 2 * D_A + 2 * B_DK + 2 * B_DV]
    h = x
    for l in range(DEPTH):
        y = rmsnorm(h, pre_mix_g[l])
        z = y @ w_in[l]
        zu, zv, zq, zk, zvb, zg, zlr = jnp.split(z, cuts, axis=-1)
        a = gmlp_mixer(zu, zv, gmlp_ln_g[l], gmlp_ln_b[l], gmlp_ws[l], gmlp_bs[l], gmlp_out_g[l])
        b = gla_mixer(zq, zk, zvb, zg, zlr, gla_gate_w[l], gla_gate_b[l], gla_out_g[l])
        m = jnp.concatenate([a, b], axis=-1) @ w_out[l]
        h = h + rmsnorm(m, post_mix_g[l])
        f = rmsnorm(h, pre_ff_g[l]) @ w_ff1[l]
        f = jnp.square(jax.nn.relu(f)) @ w_ff2[l]
        h = h + rmsnorm(f, post_ff_g[l])
    return h
```

```python
from contextlib import ExitStack

import os
import numpy as np
import concourse.bass as bass
import concourse.mybir as mybir
from concourse.bass_utils import run_bass_kernel_spmd

AF = mybir.ActivationFunctionType
ALU = mybir.AluOpType
F32 = mybir.dt.float32
BF16 = mybir.dt.bfloat16

D = 1024
DIN = 2576
DFF = 4096
T = 512
NB = 4
EPS = 1e-6
STOP = int(os.environ.get('KSTOP', '99'))
DBG = os.environ.get('KDBG', '')
GS = int(os.environ.get('KGS', '99'))
NSLOT = 3


class Op:
    __slots__ = ("eng", "fn", "deps", "chan", "grp", "sigval", "sigchan", "waits", "is_dma")


class Rec:
    def __init__(self, strict_same=True):
        self.ops = []
        self.last_w = {}
        self.readers = {}
        self.strict_same = strict_same

    def add(self, eng, fn, r=(), w=(), chan=None, grp=None):
        i = len(self.ops)
        deps = {}
        for k in r:
            j = self.last_w.get(k)
            if j is not None:
                deps[j] = True
        for k in w:
            j = self.last_w.get(k)
            if j is not None and j not in deps:
                deps[j] = False
            for j in self.readers.get(k, {}).values():
                if j not in deps:
                    deps[j] = False
        for k in r:
            rk = ("dma", i) if chan is not None else eng
            self.readers.setdefault(k, {})[rk] = i
        for k in w:
            self.last_w[k] = i
            self.readers[k] = {}
        op = Op()
        op.eng = eng
        op.fn = fn
        op.chan = chan
        op.is_dma = chan is not None
        op.grp = grp
        op.deps = [(j, raw) for j, raw in deps.items() if j != i]
        op.sigval = None
        op.sigchan = None
        op.waits = []
        self.ops.append(op)
        return i

    def _needs_sync(self, pj, pi, raw):
        if pj.grp is not None and pj.grp == pi.grp:
            return False
        if pj.is_dma:
            return True
        if pj.eng != pi.eng:
            return True
        if pi.is_dma:
            return True
        if pj.eng == "tensor":
            return False
        return bool(raw and self.strict_same)

    def emit(self, nc, block, stack):
        ops = self.ops
        need_sig = [False] * len(ops)
        for op in ops:
            for j, raw in op.deps:
                if self._needs_sync(ops[j], op, raw):
                    need_sig[j] = True
                    op.waits.append(j)
        cnt = {}
        for i, op in enumerate(ops):
            if op.is_dma:
                cnt[op.chan] = cnt.get(op.chan, 0) + 16
                op.sigchan, op.sigval = op.chan, cnt[op.chan]
            elif need_sig[i]:
                cnt[op.eng] = cnt.get(op.eng, 0) + 1
                op.sigchan, op.sigval = op.eng, cnt[op.eng]
        sems = {}
        for ch in cnt:
            nm = ch if isinstance(ch, str) else "_".join(str(c) for c in ch)
            sems[ch] = stack.enter_context(nc.semaphore("s_" + nm))
        dma_final = {ch: v for ch, v in cnt.items() if ch not in ("tensor", "vector", "scalar", "gpsimd", "sync")}

        def run(eng_name):
            def body(e):
                waited = {}
                for op in ops:
                    if op.eng != eng_name:
                        continue
                    req = {}
                    for j in op.waits:
                        ch, v = ops[j].sigchan, ops[j].sigval
                        if v > req.get(ch, 0):
                            req[ch] = v
                    for ch, v in req.items():
                        if waited.get(ch, 0) >= v:
                            continue
                        e.wait_ge(sems[ch], v)
                        waited[ch] = v
                    ins = op.fn(e)
                    if op.sigval is not None:
                        ins.then_inc(sems[op.sigchan], 16 if op.is_dma else 1)
                if eng_name == "sync":
                    for ch, v in dma_final.items():
                        if waited.get(ch, 0) < v:
                            e.wait_ge(sems[ch], v)
            return body

        block.sync(run("sync"))
        block.gpsimd(run("gpsimd"))
        block.tensor(run("tensor"))
        block.vector(run("vector"))
        block.scalar(run("scalar"))


def build_program(n_tiles, n_layers, strict_same=True):
    L = n_layers
    S = n_tiles * T
    nc = bass.Bass("TRN2", target_bir_lowering=False)
    dt = nc.dram_tensor
    x_d = dt("x", [S, D], F32, kind="ExternalInput").ap()
    y_d = dt("y", [S, D], F32, kind="ExternalOutput").ap()
    win_d = dt("w_in", [L, D, DIN], F32, kind="ExternalInput").ap()
    wout_d = dt("w_out", [L, D, D], F32, kind="ExternalInput").ap()
    w1_d = dt("w_ff1", [L, D, DFF], F32, kind="ExternalInput").ap()
    w2_d = dt("w_ff2", [L, DFF, D], F32, kind="ExternalInput").ap()
    gv_d = dt("gv", [128, L * 4 * 8], F32, kind="ExternalInput").ap()
    hv_d = dt("hv", [128, L * 3 * 4], F32, kind="ExternalInput").ap()
    lnb_d = dt("lnb_bc", [128, L * 512], F32, kind="ExternalInput").ap()
    bs_d = dt("bs_row", [1, L * 512], F32, kind="ExternalInput").ap()
    wsT_d = dt("wsT", [128, L * 512], F32, kind="ExternalInput").ap()
    gwb_d = dt("gwb", [17, L * 256], F32, kind="ExternalInput").ap()
    cst_d = dt("consts", [128, 512], F32, kind="ExternalInput").ap()

    rec = Rec(strict_same=strict_same)
    stack = ExitStack()
    with stack:
        def sb(name, shape, dtype):
            return stack.enter_context(nc.sbuf_tensor(name, shape, dtype))

        def psum(name, shape, dtype):
            return stack.enter_context(nc.psum_tensor(name, shape, dtype))

        hT = sb("hT", [128, 8, T], F32)
        yT = sb("yT", [128, 8, T], BF16)
        sq = sb("sq", [128, 8, T], BF16)
        mT = sb("mT", [128, 8, T], F32)
        big = sb("big", [128, 32, T], BF16)
        wsl = [sb(f"wslot{k}", [128, 8192], BF16) for k in range(NSLOT)]
        rstd = sb("rstd", [128, T], F32)
        rstd2 = sb("rstd2", [128, T], F32)
        apre = sb("apre", [128, 4, T], F32)
        gpre = sb("gpre", [128, 4, T], F32)
        vg32 = [sb(f"vg32_{i}", [128, 512], F32) for i in range(2)]
        vn = [sb(f"vn_{i}", [128, 512], BF16) for i in range(2)]
        bnst = [sb(f"bnst_{i}", [128, 8], F32) for i in range(2)]
        mix = [sb(f"mix_{i}", [128, 512], F32) for i in range(2)]
        e1 = [sb("e1_0", [128, 256], F32)] * 2
        sp = [sb("sp_0", [128, 256], F32)] * 2
        cumT = [sb(f"cumT_{i}", [128, 256], F32) for i in range(2)]
        E1 = [sb(f"E1_{i}", [128, 256], F32) for i in range(2)]
        E2 = [sb("E2_0", [128, 256], F32)] * 2
        E3 = [sb("E3_0", [128, 256], F32)] * 2
        qeT = [sb(f"qeT_{i}", [128, 256], BF16) for i in range(2)]
        keT = [sb(f"keT_{i}", [128, 256], BF16) for i in range(2)]
        kdT = [sb(f"kdT_{i}", [128, 256], BF16) for i in range(2)]
        kd = [sb(f"kd_{i}", [128, 256], BF16) for i in range(2)]
        scm = [sb(f"scm_{i}", [128, 512], BF16) for i in range(2)]
        zlr = sb("zlr", [32, T], BF16)
        S32 = sb("S32", [128, L, 256], F32)
        Sb = sb("Sb", [128, L, 256], BF16)
        gv = sb("gv_s", [128, L * 32], F32)
        hv = sb("hv_s", [128, L * 12], F32)
        assert L <= 4
        mflat = mT[:].rearrange("p c t -> p (c t)")
        lnb = mflat[:, 0:L * 512]
        wm32 = mflat[:, 2048:2048 + L * 512]
        bsr = apre[:].rearrange("p c t -> p (c t)")[0:1, 0:L * 512]
        wmb = sb("wmb", [128, L * 512], BF16)
        Cc = sb("Cc", [128, L * 512], F32)
        gwb = sb("gwb_s", [17, L * 256], BF16)
        cst = sb("cst", [128, 512], F32)
        identb = sb("identb", [128, 128], BF16)
        triNb = sb("triNb", [128, 128], BF16)
        sphi = sb("sphi", [128, 256], BF16)
        splo = sb("splo", [128, 256], BF16)
        onesD = sb("onesD", [128, 128], BF16)
        ones512 = sb("ones512", [128, 128], BF16)
        ones128 = sb("ones128", [128, 128], BF16)
        onerow = sb("onerow", [1, 128], F32)
        epsc = sb("epsc", [128, 1], F32)

        PS = [psum(f"ps{i}", [128, 512], F32) for i in range(7)]
        PSB = psum("psb", [128, 1024], BF16)

        ident32 = cst[:, 0:128]
        triN = cst[:, 128:256]
        tri01 = cst[:, 256:384]
        gmask = cst[:, 384:512]

        mmrot = [0]

        def mm_bank():
            b = mmrot[0] % 3
            mmrot[0] += 1
            return b

        def MM(out, lhsT, rhs, start, stop, r, w):
            rec.add("tensor", lambda e, o=out, l=lhsT, rr=rhs, s=start, p=stop: e.matmul(o, l, rr, start=s, stop=p), r=r, w=w)

        def TR(out, in_, ident, r, w):
            rec.add("tensor", lambda e, o=out, i=in_, d=ident: e.transpose(o, i, d), r=r, w=w)

        def ACT(out, in_, func, r, w, bias=None, scale=None):
            def fn(e, o=out, i=in_, f=func, b=bias, s=scale):
                kw = {}
                if b is not None:
                    kw["bias"] = b
                if s is not None:
                    kw["scale"] = s
                return e.activation(o, i, f, **kw)
            rec.add("scalar", fn, r=r, w=w)

        def V(fn, r, w):
            rec.add("vector", fn, r=r, w=w)

        def STT(out, in0, scalar, in1, op0, op1, r, w):
            V(lambda e, o=out, a=in0, s=scalar, b=in1, p0=op0, p1=op1: e.scalar_tensor_tensor(o, a, s, b, p0, p1), r, w)

        def TT(out, in0, in1, op, r, w):
            V(lambda e, o=out, a=in0, b=in1, p=op: e.tensor_tensor(o, a, b, p), r, w)

        def TS(out, in0, s1, s2, op0, op1, r, w):
            V(lambda e, o=out, a=in0, x1=s1, x2=s2, p0=op0, p1=op1: e.tensor_scalar(o, a, x1, x2, p0, p1), r, w)

        def VCOPY(out, in_, r, w):
            V(lambda e, o=out, i=in_: e.tensor_copy(o, i), r, w)

        def DMA(eng, out, in_, chan, r, w, grp=None):
            rec.add(eng, lambda e, o=out, i=in_: e.dma_start(out=o, in_=i), r=r, w=w, chan=chan, grp=grp)

        MK = [("mT", c) for c in range(8)]
        AK = [("apre", b) for b in range(NB)]
        DMA("sync", cst[:], cst_d[:, :], "ld0", [], ["cst"])
        DMA("sync", gv[:], gv_d[:, :], "ld1", [], ["gv"])
        DMA("sync", hv[:], hv_d[:, :], "ld2", [], ["hv"])
        DMA("sync", lnb, lnb_d[:, :], "ld3", [], MK)
        DMA("sync", bsr, bs_d[:, :], "ld4", [], AK)
        DMA("sync", wm32, wsT_d[:, :], "ld5", MK, MK)
        DMA("gpsimd", gwb[:], gwb_d[:, :], "ld6", [], ["gwb"])
        V(lambda e: e.memset(onesD[:], 1.0 / 1024.0), [], ["onesD"])
        V(lambda e: e.memset(ones512[:], 1.0 / 512.0), [], ["ones512"])
        V(lambda e: e.memset(ones128[:], 1.0 / 128.0), [], ["ones128"])
        V(lambda e: e.memset(onerow[:], 1.0), [], ["onerow"])
        V(lambda e: e.memset(epsc[:], EPS), [], ["epsc"])
        V(lambda e: e.memset(zlr[:], 1.0), [], ["zlr"])
        V(lambda e: e.memset(S32[:], 0.0), [], [("S32", l) for l in range(L)])
        V(lambda e: e.memset(Sb[:], 0.0), [], [("Sb", l) for l in range(L)])
        VCOPY(identb[:], ident32, ["cst"], ["identb"])
        VCOPY(triNb[:], triN, ["cst"], ["triNb"])
        for l in range(L):
            for h in range(4):
                sl = slice(l * 512 + h * 128, l * 512 + (h + 1) * 128)
                TT(wm32[:, sl], wm32[:, sl], gmask, ALU.mult, MK + ["cst"], MK)
        VCOPY(wmb[:], wm32, MK, ["wmb"])
        for l in range(L if 'nocmm' not in DBG else 0):
            bk = mm_bank()
            for h in range(4):
                sl = slice(l * 512 + h * 128, l * 512 + (h + 1) * 128)
                o = PS[bk][:, h * 128:(h + 1) * 128]
                if 'c_b' not in DBG:
                    MM(o, lnb[:, sl], wm32[:, sl], True, 'c_a' in DBG, MK, [("ps", bk)])
                if 'c_a' not in DBG:
                    MM(o, onerow[0:1, :], bsr[0:1, sl], 'c_b' in DBG, True, ["onerow"] + AK, [("ps", bk)])
            if 'c_nocopy' not in DBG:
                VCOPY(Cc[:, l * 512:(l + 1) * 512], PS[bk][:], [("ps", bk)], ["Cc"])

        slices = []
        for t in range(n_tiles):
            for l in range(L):
                slices += [(l, "in", 1), (l, "in", 2), (l, "in", 0), (l, "out", 0)]
                slices += [(l, "w1", i) for i in range(4)]
                slices += [(l, "w2", i) for i in range(4)]
        issued = [0]

        def issue_load(n):
            l, kind, i = slices[n]
            k = n % NSLOT
            slot = wsl[k]
            key = ("wslot", k)
            if kind == "in":
                c0 = i * 1024
                c1 = min(DIN, c0 + 1024)
                w = c1 - c0
                src = win_d[l].rearrange("(c p) n -> p c n", p=128)[:, :, c0:c1]
                dst = slot[:, 0:8 * w].rearrange("p (c n) -> p c n", c=8)
                DMA("gpsimd", dst, src, ("w", k), [], [key])
            elif kind == "out":
                src = wout_d[l].rearrange("(c p) n -> p c n", p=128)
                dst = slot[:, 0:8192].rearrange("p (c n) -> p c n", c=8)
                DMA("gpsimd", dst, src, ("w", k), [], [key])
            elif kind == "w1":
                src = w1_d[l].rearrange("(c p) n -> p c n", p=128)[:, :, i * 1024:(i + 1) * 1024]
                dst = slot[:, 0:8192].rearrange("p (c n) -> p c n", c=8)
                DMA("gpsimd", dst, src, ("w", k), [], [key])
            else:
                srcall = w2_d[l].rearrange("(c p) n -> p c n", p=128)[:, :, i * 256:(i + 1) * 256]
                dstall = slot[:, 0:8192].rearrange("p (c n) -> p c n", c=32)
                for q in range(4):
                    DMA("gpsimd", dstall[:, q * 8:(q + 1) * 8, :], srcall[:, q * 8:(q + 1) * 8, :], ("w", k), [], [key],
                        grp=("wfill", n))

        cur = [0]

        def next_slice(hold=0):
            n = cur[0]
            cur[0] += 1
            while issued[0] < min(len(slices), n + NSLOT - hold):
                issue_load(issued[0])
                issued[0] += 1
            return n % NSLOT

        def gcol(l, k, c):
            j = (l * 4 + k) * 8 + c
            return gv[:, j:j + 1]

        def hcol(l, k, h):
            j = (l * 3 + k) * 4 + h
            return hv[:, j:j + 1]

        def stats_rstd(sq_ap_fn, nchunk, ones_t, ones_key, sq_keys, out_rstd, out_key):
            for c in range(nchunk):
                MM(PS[3][:], ones_t[:], sq_ap_fn(c), c == 0, c == nchunk - 1, [sq_keys[c], ones_key], [("ps", 3)])
            if 'sqrt' in DBG:
                ACT(out_rstd[:], PS[3][:], AF.Sqrt, [("ps", 3), "epsc"], [out_key], bias=epsc[:, 0:1])
                V(lambda e, o=out_rstd: e.reciprocal(o[:], o[:]), [out_key], [out_key])
                return
            ACT(out_rstd[:], PS[3][:], AF.Ln, [("ps", 3), "epsc"], [out_key], bias=epsc[:, 0:1])
            ACT(out_rstd[:], out_rstd[:], AF.Exp, [out_key], [out_key], scale=-0.5)

        def pre_norm(l, k):
            for c in range(8):
                ACT(sq[:, c, :], hT[:, c, :], AF.Square, [("hT", c)], [("sq", c)])
            stats_rstd(lambda c: sq[:, c, :], 8, onesD, "onesD", [("sq", c) for c in range(8)], rstd, "rstd")
            for c in range(8):
                STT(yT[:, c, :], hT[:, c, :], gcol(l, k, c), rstd[:], ALU.mult, ALU.mult,
                    [("hT", c), "rstd", "gv"], [("yT", c)])

        def post_norm_residual(l, k):
            stats_rstd(lambda c: sq[:, c, :], 8, onesD, "onesD", [("sq", c) for c in range(8)], rstd, "rstd")
            for c in range(8):
                STT(mT[:, c, :], mT[:, c, :], gcol(l, k, c), rstd[:], ALU.mult, ALU.mult,
                    [("mT", c), "rstd", "gv"], [("mT", c)])
                TT(hT[:, c, :], hT[:, c, :], mT[:, c, :], ALU.add, [("hT", c), ("mT", c)], [("hT", c)])

        def evac_m(bk, c):
            ACT(mT[:, c, :], PS[bk][:], AF.Copy, [("ps", bk)], [("mT", c)])
            ACT(sq[:, c, :], PS[bk][:], AF.Square, [("ps", bk)], [("sq", c)])

        U0, G0, CAT0, VG0, Q0, K0 = 0, 4, 8, 16, 20, 24

        qT = sb("qT", [128, 2, T], F32)
        kT = sb("kT", [128, 2, T], F32)

        def layer(t, l):
            pre_norm(l, 0)
            k = next_slice()
            W = wsl[k][:, 0:8192].rearrange("p (c n) -> p c n", c=8)
            wkey = ("wslot", k)
            for qc in range(2):
                bk = mm_bank()
                for kc in range(8):
                    MM(PS[bk][:], W[:, kc, qc * 128:(qc + 1) * 128], yT[:, kc, :], kc == 0, kc == 7,
                       [wkey, ("yT", kc)], [("ps", bk)])
                VCOPY(qT[:, qc, :], PS[bk][:], [("ps", bk)], [("qT", qc)])
            for qc in range(2):
                bk = mm_bank()
                for kc in range(8):
                    MM(PS[bk][:], W[:, kc, 256 + qc * 128:256 + (qc + 1) * 128], yT[:, kc, :], kc == 0, kc == 7,
                       [wkey, ("yT", kc)], [("ps", bk)])
                VCOPY(kT[:, qc, :], PS[bk][:], [("ps", bk)], [("kT", qc)])
            for b in range(NB):
                bk = mm_bank()
                for kc in range(8):
                    MM(PS[bk][:], yT[:, kc, b * 128:(b + 1) * 128], W[:, kc, 512:1024], kc == 0, kc == 7,
                       [wkey, ("yT", kc)], [("ps", bk)])
                VCOPY(big[:, VG0 + b, :], PS[bk][:], [("ps", bk)], [("big", VG0 + b)])
            kB = next_slice()
            WB = wsl[kB][:, 0:8 * 528].rearrange("p (c n) -> p c n", c=8)
            wkeyB = ("wslot", kB)
            bk = mm_bank()
            for kc in range(8):
                MM(PS[bk][0:16, :], WB[:, kc, 512:528], yT[:, kc, :], kc == 0, kc == 7,
                   [wkeyB, ("yT", kc)], [("ps", bk)])
            VCOPY(zlr[0:16, :], PS[bk][0:16, :], [("ps", bk)], ["zlr"])
            kC = next_slice(hold=1)
            WC = wsl[kC][:, 0:8192].rearrange("p (c n) -> p c n", c=8)
            wkeyC = ("wslot", kC)

            def f_g(gc):
                def fn():
                    bk = mm_bank()
                    for kc in range(8):
                        MM(PS[bk][:], WB[:, kc, gc * 128:(gc + 1) * 128], yT[:, kc, :], kc == 0, kc == 7,
                           [wkeyB, ("yT", kc)], [("ps", bk)])
                    ACT(big[:, G0 + gc, :], PS[bk][:], AF.Silu, [("ps", bk)], [("big", G0 + gc)])
                return fn

            def f_u(uc):
                def fn():
                    bk = mm_bank()
                    for kc in range(8):
                        MM(PS[bk][:], WC[:, kc, uc * 128:(uc + 1) * 128], yT[:, kc, :], kc == 0, kc == 7,
                           [wkeyC, ("yT", kc)], [("ps", bk)])
                    ACT(big[:, U0 + uc, :], PS[bk][:], AF.Gelu_apprx_tanh, [("ps", bk)], [("big", U0 + uc)])
                return fn

            def f_v(b):
                def fn():
                    i2 = b % 2
                    bk = mm_bank()
                    for kc in range(8):
                        MM(PS[bk][:], yT[:, kc, b * 128:(b + 1) * 128], WC[:, kc, 512:1024], kc == 0, kc == 7,
                           [wkeyC, ("yT", kc)], [("ps", bk)])
                    ACT(vg32[i2][:], PS[bk][:], AF.Gelu_apprx_tanh, [("ps", bk)], [("vg32", i2)])
                    V(lambda e, o=bnst[i2], i=vg32[i2]: e.bn_stats(o[:, 0:6], i[:]), [("vg32", i2)], [("bnst", i2)])
                    V(lambda e, o=bnst[i2]: e.bn_aggr(o[:, 6:8], o[:, 0:6]), [("bnst", i2)], [("bnst", i2)])
                    ACT(bnst[i2][:, 7:8], bnst[i2][:, 7:8], AF.Ln, [("bnst", i2), "epsc"], [("bnst", i2)], bias=epsc[:, 0:1])
                    ACT(bnst[i2][:, 7:8], bnst[i2][:, 7:8], AF.Exp, [("bnst", i2)], [("bnst", i2)], scale=-0.5)
                    TS(vn[i2][:], vg32[i2][:], bnst[i2][:, 6:7], bnst[i2][:, 7:8], ALU.subtract, ALU.mult,
                       [("vg32", i2), ("bnst", i2)], [("vn", i2)])
                return fn

            def f_p(b):
                def fn():
                    i2 = b % 2
                    bk2 = mm_bank()
                    for h in range(4):
                        sl = slice(l * 512 + h * 128, l * 512 + (h + 1) * 128)
                        MM(PS[bk2][:, h * 128:(h + 1) * 128], vn[i2][:, h * 128:(h + 1) * 128], wmb[:, sl], True, True,
                           [("vn", i2), "wmb"], [("ps", bk2)])
                    for h in range(4):
                        sl = slice(l * 512 + h * 128, l * 512 + (h + 1) * 128)
                        STT(mix[i2][:, h * 128:(h + 1) * 128], PS[bk2][:, h * 128:(h + 1) * 128], hcol(l, 0, h), Cc[:, sl],
                            ALU.mult, ALU.add, [("ps", bk2), "hv", "Cc"], [("mix", i2)])
                    TT(gpre[:, :, b * 128:(b + 1) * 128], mix[i2][:].rearrange("p (h i) -> p h i", h=4),
                       big[:, U0:U0 + 4, b * 128:(b + 1) * 128], ALU.mult,
                       [("mix", i2)] + [("big", U0 + h) for h in range(4)], [("gpre", b)])
                return fn

            fillers = [f_u(0), f_u(1), f_u(2), f_u(3)]
            for b in range(NB):
                fillers += [f_v(b), f_p(b)]

            def gla_block(b):
                i2 = b % 2
                bsl = slice(b * 128, (b + 1) * 128)
                MM(PS[4][:, 0:256], zlr[0:17, bsl], gwb[0:17, l * 256:(l + 1) * 256], True, True,
                   ["zlr", "gwb"], [("ps", 4)])
                ACT(e1[i2][:], PS[4][:, 0:256], AF.Exp, [("ps", 4)], [("e1", 0)], scale=-1.0)
                ACT(sp[i2][:], e1[i2][:], AF.Ln, [("e1", 0)], [("sp", 0)], bias=1.0)
                VCOPY(sphi[:], sp[i2][:], [("sp", 0)], ["sphi"])
                TT(splo[:], sp[i2][:], sphi[:], ALU.subtract, [("sp", 0), "sphi"], ["splo"])
                yield
                bkc = mm_bank()
                for dc in range(2):
                    o = PS[bkc][:, dc * 128:(dc + 1) * 128]
                    MM(o, sphi[:, dc * 128:(dc + 1) * 128], triNb[:], True, False, ["sphi", "triNb"], [("ps", bkc)])
                    MM(o, splo[:, dc * 128:(dc + 1) * 128], triNb[:], False, True, ["splo", "triNb"], [("ps", bkc)])
                VCOPY(cumT[i2][:], PS[bkc][:, 0:256], [("ps", bkc)], [("cumT", i2)])
                ACT(E1[i2][:], cumT[i2][:], AF.Exp, [("cumT", i2)], [("E1", i2)])
                ACT(E2[i2][:], cumT[i2][:], AF.Exp, [("cumT", i2)], [("E2", 0)], scale=-1.0)
                for dc in range(2):
                    ACT(E3[i2][:, dc * 128:(dc + 1) * 128], cumT[i2][:, dc * 128:(dc + 1) * 128], AF.Exp,
                        [("cumT", i2)], [("E3", 0)], scale=-1.0, bias=cumT[i2][:, dc * 128 + 127:dc * 128 + 128])
                r3 = "p (c i) -> p c i"
                STT(qeT[i2][:].rearrange(r3, c=2), qT[:, :, bsl], 0.125, E1[i2][:].rearrange(r3, c=2), ALU.mult, ALU.mult,
                    [("qT", 0), ("qT", 1), ("E1", i2)], [("qeT", i2)])
                TT(keT[i2][:].rearrange(r3, c=2), kT[:, :, bsl], E2[i2][:].rearrange(r3, c=2), ALU.mult,
                   [("kT", 0), ("kT", 1), ("E2", 0)], [("keT", i2)])
                TT(kdT[i2][:].rearrange(r3, c=2), kT[:, :, bsl], E3[i2][:].rearrange(r3, c=2), ALU.mult,
                   [("kT", 0), ("kT", 1), ("E3", 0)], [("kdT", i2)])
                yield
                for dc in range(2):
                    TR(PSB[:, dc * 128:(dc + 1) * 128], kdT[i2][:, dc * 128:(dc + 1) * 128], identb[:],
                       [("kdT", i2), "identb"], [("psb", 0)])
                VCOPY(kd[i2][:], PSB[:, 0:256], [("psb", 0)], [("kd", i2)])
                bks = (5, 4)
                for h in range(4):
                    dc, par = h // 2, h % 2
                    rows = slice(par * 64, par * 64 + 64)
                    MM(PS[bks[par]][:, h * 128:(h + 1) * 128], keT[i2][rows, dc * 128:(dc + 1) * 128],
                       qeT[i2][rows, dc * 128:(dc + 1) * 128], True, True,
                       [("keT", i2), ("qeT", i2)], [("ps", bks[par])])
                for h in range(4):
                    par = h % 2
                    TT(scm[i2][:, h * 128:(h + 1) * 128], PS[bks[par]][:, h * 128:(h + 1) * 128], tri01, ALU.mult,
                       [("ps", bks[par]), "cst"], [("scm", i2)])
                yield
                for h in range(4):
                    dc, par = h // 2, h % 2
                    rows = slice(par * 64, par * 64 + 64)
                    o = PS[6][:, h * 128:(h + 1) * 128]
                    MM(o, big[:, VG0 + b, h * 128:(h + 1) * 128], scm[i2][:, h * 128:(h + 1) * 128], True, False,
                       [("big", VG0 + b), ("scm", i2)], [("ps", 6)])
                    MM(o, Sb[rows, l, dc * 128:(dc + 1) * 128], qeT[i2][rows, dc * 128:(dc + 1) * 128], False, True,
                       [("Sb", l), ("qeT", i2)], [("ps", 6)])
                VCOPY(apre[:, :, bsl], PS[6][:].rearrange("p (h i) -> p h i", h=4), [("ps", 6)], [("apre", b)])
                bk = mm_bank()
                for h in range(4):
                    dc = h // 2
                    MM(PS[bk][:, h * 128:(h + 1) * 128], kd[i2][:, dc * 128:(dc + 1) * 128],
                       big[:, VG0 + b, h * 128:(h + 1) * 128], True, True,
                       [("kd", i2), ("big", VG0 + b)], [("ps", bk)])
                for h in range(4):
                    dc, par = h // 2, h % 2
                    rows = slice(par * 64, par * 64 + 64)
                    STT(S32[rows, l, dc * 128:(dc + 1) * 128], S32[rows, l, dc * 128:(dc + 1) * 128],
                        E1[i2][rows, dc * 128 + 127:dc * 128 + 128], PS[bk][rows, h * 128:(h + 1) * 128],
                        ALU.mult, ALU.add, [("S32", l), ("E1", i2), ("ps", bk)], [("S32", l)])
                ACT(Sb[:, l, :], S32[:, l, :], AF.Copy, [("S32", l)], [("Sb", l)])
                yield

            fi = 0
            gens = [gla_block(b) for b in range(NB)]
            for kstep in range(NB + 3):
                for b in range(NB):
                    if 0 <= kstep - b < 4:
                        next(gens[b])
                        if fi < len(fillers):
                            fillers[fi]()
                            fi += 1
            while fi < len(fillers):
                fillers[fi]()
                fi += 1

            for h in range(4):
                ACT(sq[:, h, :], gpre[:, h, :], AF.Square, [("gpre", b) for b in range(NB)], [("sq", h)])
            for h in range(4):
                ACT(sq[:, 4 + h, :], apre[:, h, :], AF.Square, [("apre", b) for b in range(NB)], [("sq", 4 + h)])
            f_g(0)()
            stats_rstd(lambda c: sq[:, c, :], 4, ones512, "ones512", [("sq", c) for c in range(4)], rstd, "rstd")
            for h in range(4):
                STT(big[:, CAT0 + h, :], gpre[:, h, :], hcol(l, 1, h), rstd[:], ALU.mult, ALU.mult,
                    [("gpre", b) for b in range(NB)] + ["rstd", "hv"], [("big", CAT0 + h)])
            f_g(1)()
            for h in range(4):
                MM(PS[3][:], ones128[:], sq[:, 4 + h, :], True, True, [("sq", 4 + h), "ones128"], [("ps", 3)])
                ACT(rstd2[:], PS[3][:], AF.Ln, [("ps", 3), "epsc"], ["rstd2"], bias=epsc[:, 0:1])
                ACT(rstd2[:], rstd2[:], AF.Exp, ["rstd2"], ["rstd2"], scale=-0.5)
                STT(apre[:, h, :], apre[:, h, :], hcol(l, 2, h), rstd2[:], ALU.mult, ALU.mult,
                    [("apre", b) for b in range(NB)] + ["rstd2", "hv"], [("apre", b) for b in range(NB)])
                TT(big[:, CAT0 + 4 + h, :], apre[:, h, :], big[:, G0 + h, :], ALU.mult,
                   [("apre", b) for b in range(NB)] + [("big", G0 + h)], [("big", CAT0 + 4 + h)])
                if h + 2 < 4:
                    f_g(h + 2)()

            k = next_slice()
            W = wsl[k][:, 0:8192].rearrange("p (c n) -> p c n", c=8)
            wkey = ("wslot", k)
            for dc in range(8):
                bk = mm_bank()
                for mc in range(8):
                    MM(PS[bk][:], W[:, mc, dc * 128:(dc + 1) * 128], big[:, CAT0 + mc, :], mc == 0, mc == 7,
                       [wkey, ("big", CAT0 + mc)], [("ps", bk)])
                evac_m(bk, dc)
            post_norm_residual(l, 1)

            if STOP <= 5:
                return
            pre_norm(l, 2)
            for s in range(4):
                k = next_slice()
                W = wsl[k][:, 0:8192].rearrange("p (c n) -> p c n", c=8)
                wkey = ("wslot", k)
                for fc in range(8):
                    ffc = s * 8 + fc
                    bk = mm_bank()
                    for kc in range(8):
                        MM(PS[bk][:], W[:, kc, fc * 128:(fc + 1) * 128], yT[:, kc, :], kc == 0, kc == 7,
                           [wkey, ("yT", kc)], [("ps", bk)])
                    r2 = ffc % 2
                    ACT(vg32[r2][:], PS[bk][:], AF.Relu, [("ps", bk)], [("vg32", r2)])
                    TT(big[:, ffc, :], vg32[r2][:], vg32[r2][:], ALU.mult, [("vg32", r2)], [("big", ffc)])
            for s in range(4):
                k = next_slice()
                W = wsl[k][:, 0:8192].rearrange("p (c n) -> p c n", c=32)
                wkey = ("wslot", k)
                for dl in range(2):
                    dc = s * 2 + dl
                    bk = mm_bank()
                    for ffc in range(32):
                        MM(PS[bk][:], W[:, ffc, dl * 128:(dl + 1) * 128], big[:, ffc, :], ffc == 0, ffc == 31,
                           [wkey, ("big", ffc)], [("ps", bk)])
                    evac_m(bk, dc)
            post_norm_residual(l, 3)

        stage = mflat.rearrange("p (b d) -> p b d", b=NB)
        for t in range(n_tiles):
            DMA("sync", stage, x_d[t * T:(t + 1) * T, :].rearrange("(b p) d -> p b d", p=128), "xin",
                [], [("mT", c) for c in range(8)])
            for c in range(8 if 'notr' not in DBG else 0):
                bk = mm_bank()
                for b in range(NB):
                    TR(PS[bk][:, b * 128:(b + 1) * 128], stage[:, b, c * 128:(c + 1) * 128], ident32,
                       [("mT", 2 * b), ("mT", 2 * b + 1), "cst"], [("ps", bk)])
                ACT(hT[:, c, :], PS[bk][:], AF.Copy, [("ps", bk)], [("hT", c)])
            for l in range(L):
                layer(t, l)
            for b in range(NB if 'notr' not in DBG else 0):
                for half in range(2):
                    bk = mm_bank()
                    for cc in range(4):
                        c = half * 4 + cc
                        TR(PS[bk][:, cc * 128:(cc + 1) * 128], hT[:, c, b * 128:(b + 1) * 128], ident32,
                           [("hT", c), "cst"], [("ps", bk)])
                    ACT(stage[:, b, half * 512:(half + 1) * 512], PS[bk][:], AF.Copy, [("ps", bk)], [("mT", 2 * b + half)])
            DMA("sync", y_d[t * T:(t + 1) * T, :].rearrange("(b p) d -> p b d", p=128), stage, "yout",
                [("mT", c) for c in range(8)], [("yout", t)])

        block = stack.enter_context(nc.Block())
        rec.emit(nc, block, stack)
    return nc


def _consts():
    c = np.zeros((128, 512), np.float32)
    j = np.arange(128)[:, None]
    i = np.arange(128)[None, :]
    c[:, 0:128] = np.eye(128, dtype=np.float32)
    c[:, 128:256] = np.where(j <= i, -1.0 / 16.0, 0.0)
    c[:, 256:384] = np.where(j <= i, 1.0, 0.0)
    c[:, 384:512] = np.where((i // 64) >= (j // 64), 1.0, 0.0)
    return c


def _prep_params(L, pre_mix_g, post_mix_g, pre_ff_g, post_ff_g, gmlp_ln_g, gmlp_ln_b, gmlp_ws, gmlp_bs,
                 gmlp_out_g, gla_gate_w, gla_gate_b, gla_out_g):
    f = np.float32
    gv = np.stack([pre_mix_g, post_mix_g, pre_ff_g, post_ff_g], axis=1).astype(f)
    gv = gv.reshape(L, 4, 8, 128).transpose(3, 0, 1, 2).reshape(128, L * 32)
    hv = np.stack([gmlp_ln_g, gmlp_out_g, gla_out_g], axis=1).astype(f)
    hv = hv.reshape(L, 3, 4, 128).transpose(3, 0, 1, 2).reshape(128, L * 12)
    lnb = np.broadcast_to(np.asarray(gmlp_ln_b, f).reshape(1, L * 512), (128, L * 512))
    bs = np.asarray(gmlp_bs, f).reshape(1, L * 512)
    wsT = np.asarray(gmlp_ws, f).transpose(3, 0, 1, 2).reshape(128, L * 512)
    gwb = np.concatenate([np.asarray(gla_gate_w, f), np.asarray(gla_gate_b, f)[:, None, :]], axis=1)
    gwb = gwb.transpose(1, 0, 2).reshape(17, L * 256)
    return dict(gv=np.ascontiguousarray(gv), hv=np.ascontiguousarray(hv), lnb_bc=np.ascontiguousarray(lnb),
                bs_row=np.ascontiguousarray(bs), wsT=np.ascontiguousarray(wsT), gwb=np.ascontiguousarray(gwb),
                consts=_consts())


def run_model(x, pre_mix_g, w_in, gmlp_ln_g, gmlp_ln_b, gmlp_ws, gmlp_bs, gmlp_out_g,
              gla_gate_w, gla_gate_b, gla_out_g, w_out, post_mix_g, pre_ff_g,
              w_ff1, w_ff2, post_ff_g, n_cores=8, strict_same=True):
    x = np.asarray(x, np.float32)
    B, S, _ = x.shape
    L = int(np.asarray(w_in).shape[0])
    n_tiles = S // T
    nc = build_program(n_tiles, L, strict_same=strict_same)
    params = _prep_params(L, np.asarray(pre_mix_g), np.asarray(post_mix_g), np.asarray(pre_ff_g), np.asarray(post_ff_g),
                          np.asarray(gmlp_ln_g), np.asarray(gmlp_ln_b), np.asarray(gmlp_ws), np.asarray(gmlp_bs),
                          np.asarray(gmlp_out_g), np.asarray(gla_gate_w), np.asarray(gla_gate_b), np.asarray(gla_out_g))
    big = dict(w_in=np.ascontiguousarray(w_in, dtype=np.float32), w_out=np.ascontiguousarray(w_out, dtype=np.float32),
               w_ff1=np.ascontiguousarray(w_ff1, dtype=np.float32), w_ff2=np.ascontiguousarray(w_ff2, dtype=np.float32))
    if n_cores == 8 and B == 4:
        owner = {0: 0, 1: 1, 4: 2, 5: 3}
    else:
        owner = {c: c for c in range(min(B, n_cores))}
    zeros = None
    in_maps = []
    for c in range(n_cores):
        if c in owner:
            m = dict(params)
            m.update(big)
            m["x"] = np.ascontiguousarray(x[owner[c]])
        else:
            if zeros is None:
                zeros = {k: np.zeros_like(v) for k, v in {**params, **big}.items()}
                zeros["x"] = np.zeros_like(x[0])
            m = dict(zeros)
        in_maps.append(m)
    res = run_bass_kernel_spmd(nc, in_maps, core_ids=list(range(n_cores)))
    inv = {b: c for c, b in owner.items()}
    out = np.stack([np.asarray(res.results[inv[b]]["y"], dtype=np.float32) for b in range(B)], axis=0)
    return out


def kernel(**inputs):
    return run_model(**inputs)
```

```python
from contextlib import ExitStack

import os
import numpy as np
import concourse.bass as bass
import concourse.mybir as mybir
from concourse.bass_utils import run_bass_kernel_spmd

AF = mybir.ActivationFunctionType
ALU = mybir.AluOpType
F32 = mybir.dt.float32
BF16 = mybir.dt.bfloat16

D = 1024
DIN = 2576
DFF = 4096
T = 512
NB = 4
EPS = 1e-6
STOP = int(os.environ.get('KSTOP', '99'))
DBG = os.environ.get('KDBG', '')
GS = int(os.environ.get('KGS', '99'))
NSLOT = 3


class Op:
    __slots__ = ("eng", "fn", "deps", "chan", "grp", "sigval", "sigchan", "waits", "is_dma")


class Rec:
    def __init__(self, strict_same=True):
        self.ops = []
        self.last_w = {}
        self.readers = {}
        self.strict_same = strict_same

    def add(self, eng, fn, r=(), w=(), chan=None, grp=None):
        i = len(self.ops)
        deps = {}
        for k in r:
            j = self.last_w.get(k)
            if j is not None:
                deps[j] = True
        for k in w:
            j = self.last_w.get(k)
            if j is not None and j not in deps:
                deps[j] = False
            for j in self.readers.get(k, {}).values():
                if j not in deps:
                    deps[j] = False
        for k in r:
            rk = ("dma", i) if chan is not None else eng
            self.readers.setdefault(k, {})[rk] = i
        for k in w:
            self.last_w[k] = i
            self.readers[k] = {}
        op = Op()
        op.eng = eng
        op.fn = fn
        op.chan = chan
        op.is_dma = chan is not None
        op.grp = grp
        op.deps = [(j, raw) for j, raw in deps.items() if j != i]
        op.sigval = None
        op.sigchan = None
        op.waits = []
        self.ops.append(op)
        return i

    def _needs_sync(self, pj, pi, raw):
        if pj.grp is not None and pj.grp == pi.grp:
            return False
        if pj.is_dma:
            return True
        if pj.eng != pi.eng:
            return True
        if pi.is_dma:
            return True
        if pj.eng == "tensor":
            return False
        return bool(raw and self.strict_same)

    def emit(self, nc, block, stack):
        ops = self.ops
        need_sig = [False] * len(ops)
        for op in ops:
            for j, raw in op.deps:
                if self._needs_sync(ops[j], op, raw):
                    need_sig[j] = True
                    op.waits.append(j)
        cnt = {}
        for i, op in enumerate(ops):
            if op.is_dma:
                cnt[op.chan] = cnt.get(op.chan, 0) + 16
                op.sigchan, op.sigval = op.chan, cnt[op.chan]
            elif need_sig[i]:
                cnt[op.eng] = cnt.get(op.eng, 0) + 1
                op.sigchan, op.sigval = op.eng, cnt[op.eng]
        sems = {}
        for ch in cnt:
            nm = ch if isinstance(ch, str) else "_".join(str(c) for c in ch)
            sems[ch] = stack.enter_context(nc.semaphore("s_" + nm))
        dma_final = {ch: v for ch, v in cnt.items() if ch not in ("tensor", "vector", "scalar", "gpsimd", "sync")}

        def run(eng_name):
            def body(e):
                waited = {}
                for op in ops:
                    if op.eng != eng_name:
                        continue
                    req = {}
                    for j in op.waits:
                        ch, v = ops[j].sigchan, ops[j].sigval
                        if v > req.get(ch, 0):
                            req[ch] = v
                    for ch, v in req.items():
                        if waited.get(ch, 0) >= v:
                            continue
                        e.wait_ge(sems[ch], v)
                        waited[ch] = v
                    ins = op.fn(e)
                    if op.sigval is not None:
                        ins.then_inc(sems[op.sigchan], 16 if op.is_dma else 1)
                if eng_name == "sync":
                    for ch, v in dma_final.items():
                        if waited.get(ch, 0) < v:
                            e.wait_ge(sems[ch], v)
            return body

        block.sync(run("sync"))
        block.gpsimd(run("gpsimd"))
        block.tensor(run("tensor"))
        block.vector(run("vector"))
        block.scalar(run("scalar"))


def build_program(n_tiles, n_layers, strict_same=True):
    L = n_layers
    S = n_tiles * T
    nc = bass.Bass("TRN2", target_bir_lowering=False)
    dt = nc.dram_tensor
    x_d = dt("x", [S, D], F32, kind="ExternalInput").ap()
    y_d = dt("y", [S, D], F32, kind="ExternalOutput").ap()
    win_d = dt("w_in", [L, D, DIN], F32, kind="ExternalInput").ap()
    wout_d = dt("w_out", [L, D, D], F32, kind="ExternalInput").ap()
    w1_d = dt("w_ff1", [L, D, DFF], F32, kind="ExternalInput").ap()
    w2_d = dt("w_ff2", [L, DFF, D], F32, kind="ExternalInput").ap()
    gv_d = dt("gv", [128, L * 4 * 8], F32, kind="ExternalInput").ap()
    hv_d = dt("hv", [128, L * 3 * 4], F32, kind="ExternalInput").ap()
    lnb_d = dt("lnb_bc", [128, L * 512], F32, kind="ExternalInput").ap()
    bs_d = dt("bs_row", [1, L * 512], F32, kind="ExternalInput").ap()
    wsT_d = dt("wsT", [128, L * 512], F32, kind="ExternalInput").ap()
    gwb_d = dt("gwb", [17, L * 256], F32, kind="ExternalInput").ap()
    cst_d = dt("consts", [128, 512], F32, kind="ExternalInput").ap()

    rec = Rec(strict_same=strict_same)
    stack = ExitStack()
    with stack:
        def sb(name, shape, dtype):
            return stack.enter_context(nc.sbuf_tensor(name, shape, dtype))

        def psum(name, shape, dtype):
            return stack.enter_context(nc.psum_tensor(name, shape, dtype))

        hT = sb("hT", [128, 8, T], F32)
        yT = sb("yT", [128, 8, T], BF16)
        sq = sb("sq", [128, 8, T], BF16)
        mT = sb("mT", [128, 8, T], F32)
        big = sb("big", [128, 32, T], BF16)
        wsl = [sb(f"wslot{k}", [128, 8192], BF16) for k in range(NSLOT)]
        rstd = sb("rstd", [128, T], F32)
        rstd2 = sb("rstd2", [128, T], F32)
        apre = sb("apre", [128, 4, T], F32)
        gpre = sb("gpre", [128, 4, T], F32)
        vg32 = [sb(f"vg32_{i}", [128, 512], F32) for i in range(2)]
        vc = sb("vc", [128, NB, 512], BF16)
        bnst4 = sb("bnst4", [128, NB, 8], F32)
        rstd4 = sb("rstd4", [128, NB], F32)
        mix = [sb("mix_0", [128, 512], F32)] * 2
        e1 = [sb("e1_0", [128, 256], F32)] * 2
        sp = [sb("sp_0", [128, 256], F32)] * 2
        cumT = [sb(f"cumT_{i}", [128, 256], F32) for i in range(2)]
        E1 = [sb(f"E1_{i}", [128, 256], F32) for i in range(2)]
        E2 = [sb("E2_0", [128, 256], F32)] * 2
        E3 = [sb("E3_0", [128, 256], F32)] * 2
        qeT = [sb(f"qeT_{i}", [128, 256], BF16) for i in range(2)]
        keT = [sb(f"keT_{i}", [128, 256], BF16) for i in range(2)]
        kdT = [sb(f"kdT_{i}", [128, 256], BF16) for i in range(2)]
        kd = [sb(f"kd_{i}", [128, 256], BF16) for i in range(2)]
        scm = [sb(f"scm_{i}", [128, 512], BF16) for i in range(2)]
        zlr = sb("zlr", [32, T], BF16)
        S32 = sb("S32", [128, L, 256], F32)
        Sb = sb("Sb", [128, L, 256], BF16)
        gv = sb("gv_s", [128, L * 32], F32)
        hv = sb("hv_s", [128, L * 12], F32)
        assert L <= 4
        mflat = mT[:].rearrange("p c t -> p (c t)")
        lnb = mflat[:, 0:L * 512]
        wm32 = mflat[:, 2048:2048 + L * 512]
        bsr = apre[:].rearrange("p c t -> p (c t)")[0:1, 0:L * 512]
        wmb = sb("wmb", [128, L * 512], BF16)
        Cc = sb("Cc", [128, L * 512], F32)
        gwb = sb("gwb_s", [17, L * 256], BF16)
        cst = sb("cst", [128, 512], F32)
        identb = sb("identb", [128, 128], BF16)
        triNb = sb("triNb", [128, 128], BF16)
        sphi = sb("sphi", [128, 256], BF16)
        splo = sb("splo", [128, 256], BF16)
        onesD = sb("onesD", [128, 128], BF16)
        ones512 = sb("ones512", [128, 128], BF16)
        ones128 = sb("ones128", [128, 128], BF16)
        onerow = sb("onerow", [1, 128], F32)
        epsc = sb("epsc", [128, 1], F32)

        PS = [psum(f"ps{i}", [128, 512], F32) for i in range(7)]
        PSB = psum("psb", [128, 1024], BF16)

        ident32 = cst[:, 0:128]
        triN = cst[:, 128:256]
        tri01 = cst[:, 256:384]
        gmask = cst[:, 384:512]

        mmrot = [0]

        def mm_bank():
            b = mmrot[0] % 3
            mmrot[0] += 1
            return b

        def MM(out, lhsT, rhs, start, stop, r, w):
            rec.add("tensor", lambda e, o=out, l=lhsT, rr=rhs, s=start, p=stop: e.matmul(o, l, rr, start=s, stop=p), r=r, w=w)

        def TR(out, in_, ident, r, w):
            rec.add("tensor", lambda e, o=out, i=in_, d=ident: e.transpose(o, i, d), r=r, w=w)

        def ACT(out, in_, func, r, w, bias=None, scale=None):
            def fn(e, o=out, i=in_, f=func, b=bias, s=scale):
                kw = {}
                if b is not None:
                    kw["bias"] = b
                if s is not None:
                    kw["scale"] = s
                return e.activation(o, i, f, **kw)
            rec.add("scalar", fn, r=r, w=w)

        def V(fn, r, w):
            rec.add("vector", fn, r=r, w=w)

        def STT(out, in0, scalar, in1, op0, op1, r, w):
            V(lambda e, o=out, a=in0, s=scalar, b=in1, p0=op0, p1=op1: e.scalar_tensor_tensor(o, a, s, b, p0, p1), r, w)

        def TT(out, in0, in1, op, r, w):
            V(lambda e, o=out, a=in0, b=in1, p=op: e.tensor_tensor(o, a, b, p), r, w)

        def TS(out, in0, s1, s2, op0, op1, r, w):
            V(lambda e, o=out, a=in0, x1=s1, x2=s2, p0=op0, p1=op1: e.tensor_scalar(o, a, x1, x2, p0, p1), r, w)

        def VCOPY(out, in_, r, w):
            V(lambda e, o=out, i=in_: e.tensor_copy(o, i), r, w)

        def DMA(eng, out, in_, chan, r, w, grp=None):
            rec.add(eng, lambda e, o=out, i=in_: e.dma_start(out=o, in_=i), r=r, w=w, chan=chan, grp=grp)

        MK = [("mT", c) for c in range(8)]
        AK = [("apre", b) for b in range(NB)]
        DMA("sync", cst[:], cst_d[:, :], "ld0", [], ["cst"])
        DMA("sync", gv[:], gv_d[:, :], "ld1", [], ["gv"])
        DMA("sync", hv[:], hv_d[:, :], "ld2", [], ["hv"])
        DMA("sync", lnb, lnb_d[:, :], "ld3", [], MK)
        DMA("sync", bsr, bs_d[:, :], "ld4", [], AK)
        DMA("sync", wm32, wsT_d[:, :], "ld5", MK, MK)
        DMA("gpsimd", gwb[:], gwb_d[:, :], "ld6", [], ["gwb"])
        V(lambda e: e.memset(onesD[:], 1.0 / 1024.0), [], ["onesD"])
        V(lambda e: e.memset(ones512[:], 1.0 / 512.0), [], ["ones512"])
        V(lambda e: e.memset(ones128[:], 1.0 / 128.0), [], ["ones128"])
        V(lambda e: e.memset(onerow[:], 1.0), [], ["onerow"])
        V(lambda e: e.memset(epsc[:], EPS), [], ["epsc"])
        V(lambda e: e.memset(zlr[:], 1.0), [], ["zlr"])
        V(lambda e: e.memset(S32[:], 0.0), [], [("S32", l) for l in range(L)])
        V(lambda e: e.memset(Sb[:], 0.0), [], [("Sb", l) for l in range(L)])
        VCOPY(identb[:], ident32, ["cst"], ["identb"])
        VCOPY(triNb[:], triN, ["cst"], ["triNb"])
        for l in range(L):
            for h in range(4):
                sl = slice(l * 512 + h * 128, l * 512 + (h + 1) * 128)
                TT(wm32[:, sl], wm32[:, sl], gmask, ALU.mult, MK + ["cst"], MK)
        VCOPY(wmb[:], wm32, MK, ["wmb"])
        for l in range(L if 'nocmm' not in DBG else 0):
            bk = mm_bank()
            for h in range(4):
                sl = slice(l * 512 + h * 128, l * 512 + (h + 1) * 128)
                o = PS[bk][:, h * 128:(h + 1) * 128]
                if 'c_b' not in DBG:
                    MM(o, lnb[:, sl], wm32[:, sl], True, 'c_a' in DBG, MK, [("ps", bk)])
                if 'c_a' not in DBG:
                    MM(o, onerow[0:1, :], bsr[0:1, sl], 'c_b' in DBG, True, ["onerow"] + AK, [("ps", bk)])
            if 'c_nocopy' not in DBG:
                VCOPY(Cc[:, l * 512:(l + 1) * 512], PS[bk][:], [("ps", bk)], ["Cc"])

        slices = []
        for t in range(n_tiles):
            for l in range(L):
                slices += [(l, "in", 1), (l, "in", 2), (l, "in", 0), (l, "out", 0)]
                slices += [(l, "w1", i) for i in range(4)]
                slices += [(l, "w2", i) for i in range(4)]
        issued = [0]

        def issue_load(n):
            l, kind, i = slices[n]
            k = n % NSLOT
            slot = wsl[k]
            key = ("wslot", k)
            if kind == "in":
                c0 = i * 1024
                c1 = min(DIN, c0 + 1024)
                w = c1 - c0
                src = win_d[l].rearrange("(c p) n -> p c n", p=128)[:, :, c0:c1]
                dst = slot[:, 0:8 * w].rearrange("p (c n) -> p c n", c=8)
                DMA("gpsimd", dst, src, ("w", k), [], [key])
            elif kind == "out":
                src = wout_d[l].rearrange("(c p) n -> p c n", p=128)
                dst = slot[:, 0:8192].rearrange("p (c n) -> p c n", c=8)
                DMA("gpsimd", dst, src, ("w", k), [], [key])
            elif kind == "w1":
                src = w1_d[l].rearrange("(c p) n -> p c n", p=128)[:, :, i * 1024:(i + 1) * 1024]
                dst = slot[:, 0:8192].rearrange("p (c n) -> p c n", c=8)
                DMA("gpsimd", dst, src, ("w", k), [], [key])
            else:
                srcall = w2_d[l].rearrange("(c p) n -> p c n", p=128)[:, :, i * 256:(i + 1) * 256]
                dstall = slot[:, 0:8192].rearrange("p (c n) -> p c n", c=32)
                for q in range(4):
                    DMA("gpsimd", dstall[:, q * 8:(q + 1) * 8, :], srcall[:, q * 8:(q + 1) * 8, :], ("w", k), [], [key],
                        grp=("wfill", n))

        cur = [0]

        def next_slice(hold=0):
            n = cur[0]
            cur[0] += 1
            while issued[0] < min(len(slices), n + NSLOT - hold):
                issue_load(issued[0])
                issued[0] += 1
            return n % NSLOT

        def gcol(l, k, c):
            j = (l * 4 + k) * 8 + c
            return gv[:, j:j + 1]

        def hcol(l, k, h):
            j = (l * 3 + k) * 4 + h
            return hv[:, j:j + 1]

        def stats_rstd(sq_ap_fn, nchunk, ones_t, ones_key, sq_keys, out_rstd, out_key):
            for c in range(nchunk):
                MM(PS[3][:], ones_t[:], sq_ap_fn(c), c == 0, c == nchunk - 1, [sq_keys[c], ones_key], [("ps", 3)])
            if 'sqrt' in DBG:
                ACT(out_rstd[:], PS[3][:], AF.Sqrt, [("ps", 3), "epsc"], [out_key], bias=epsc[:, 0:1])
                V(lambda e, o=out_rstd: e.reciprocal(o[:], o[:]), [out_key], [out_key])
                return
            ACT(out_rstd[:], PS[3][:], AF.Ln, [("ps", 3), "epsc"], [out_key], bias=epsc[:, 0:1])
            ACT(out_rstd[:], out_rstd[:], AF.Exp, [out_key], [out_key], scale=-0.5)

        def pre_norm(l, k):
            for c in range(8):
                ACT(sq[:, c, :], hT[:, c, :], AF.Square, [("hT", c)], [("sq", c)])
            stats_rstd(lambda c: sq[:, c, :], 8, onesD, "onesD", [("sq", c) for c in range(8)], rstd, "rstd")
            for c in range(8):
                STT(yT[:, c, :], hT[:, c, :], gcol(l, k, c), rstd[:], ALU.mult, ALU.mult,
                    [("hT", c), "rstd", "gv"], [("yT", c)])

        def post_norm_residual(l, k):
            stats_rstd(lambda c: sq[:, c, :], 8, onesD, "onesD", [("sq", c) for c in range(8)], rstd, "rstd")
            for c in range(8):
                STT(mT[:, c, :], mT[:, c, :], gcol(l, k, c), rstd[:], ALU.mult, ALU.mult,
                    [("mT", c), "rstd", "gv"], [("mT", c)])
                TT(hT[:, c, :], hT[:, c, :], mT[:, c, :], ALU.add, [("hT", c), ("mT", c)], [("hT", c)])

        def evac_m(bk, c):
            ACT(mT[:, c, :], PS[bk][:], AF.Copy, [("ps", bk)], [("mT", c)])
            ACT(sq[:, c, :], PS[bk][:], AF.Square, [("ps", bk)], [("sq", c)])

        U0, G0, CAT0, VG0, Q0, K0 = 0, 4, 8, 16, 20, 24

        qT = sb("qT", [128, 2, T], F32)
        kT = sb("kT", [128, 2, T], F32)

        def layer(t, l):
            pre_norm(l, 0)
            k = next_slice()
            W = wsl[k][:, 0:8192].rearrange("p (c n) -> p c n", c=8)
            wkey = ("wslot", k)
            for qc in range(2):
                bk = mm_bank()
                for kc in range(8):
                    MM(PS[bk][:], W[:, kc, qc * 128:(qc + 1) * 128], yT[:, kc, :], kc == 0, kc == 7,
                       [wkey, ("yT", kc)], [("ps", bk)])
                VCOPY(qT[:, qc, :], PS[bk][:], [("ps", bk)], [("qT", qc)])
            for qc in range(2):
                bk = mm_bank()
                for kc in range(8):
                    MM(PS[bk][:], W[:, kc, 256 + qc * 128:256 + (qc + 1) * 128], yT[:, kc, :], kc == 0, kc == 7,
                       [wkey, ("yT", kc)], [("ps", bk)])
                VCOPY(kT[:, qc, :], PS[bk][:], [("ps", bk)], [("kT", qc)])
            for b in range(NB):
                bk = mm_bank()
                for kc in range(8):
                    MM(PS[bk][:], yT[:, kc, b * 128:(b + 1) * 128], W[:, kc, 512:1024], kc == 0, kc == 7,
                       [wkey, ("yT", kc)], [("ps", bk)])
                VCOPY(big[:, VG0 + b, :], PS[bk][:], [("ps", bk)], [("big", VG0 + b)])
            kB = next_slice()
            WB = wsl[kB][:, 0:8 * 528].rearrange("p (c n) -> p c n", c=8)
            wkeyB = ("wslot", kB)
            bk = mm_bank()
            for kc in range(8):
                MM(PS[bk][0:16, :], WB[:, kc, 512:528], yT[:, kc, :], kc == 0, kc == 7,
                   [wkeyB, ("yT", kc)], [("ps", bk)])
            VCOPY(zlr[0:16, :], PS[bk][0:16, :], [("ps", bk)], ["zlr"])
            kC = next_slice(hold=1)
            WC = wsl[kC][:, 0:8192].rearrange("p (c n) -> p c n", c=8)
            wkeyC = ("wslot", kC)

            def f_g(gc):
                def fn():
                    bk = mm_bank()
                    for kc in range(8):
                        MM(PS[bk][:], WB[:, kc, gc * 128:(gc + 1) * 128], yT[:, kc, :], kc == 0, kc == 7,
                           [wkeyB, ("yT", kc)], [("ps", bk)])
                    ACT(big[:, G0 + gc, :], PS[bk][:], AF.Silu, [("ps", bk)], [("big", G0 + gc)])
                return fn

            def f_u(uc):
                def fn():
                    bk = mm_bank()
                    for kc in range(8):
                        MM(PS[bk][:], WC[:, kc, uc * 128:(uc + 1) * 128], yT[:, kc, :], kc == 0, kc == 7,
                           [wkeyC, ("yT", kc)], [("ps", bk)])
                    ACT(big[:, U0 + uc, :], PS[bk][:], AF.Gelu_apprx_tanh, [("ps", bk)], [("big", U0 + uc)])
                return fn

            def f_v(b):
                def fn():
                    i2 = b % 2
                    bk = mm_bank()
                    for kc in range(8):
                        MM(PS[bk][:], yT[:, kc, b * 128:(b + 1) * 128], WC[:, kc, 512:1024], kc == 0, kc == 7,
                           [wkeyC, ("yT", kc)], [("ps", bk)])
                    ACT(vg32[i2][:], PS[bk][:], AF.Gelu_apprx_tanh, [("ps", bk)], [("vg32", i2)])
                    V(lambda e, o=bnst4, i=vg32[i2]: e.bn_stats(o[:, b, 0:6], i[:]), [("vg32", i2)], [("bnst", b)])
                    V(lambda e, o=bnst4: e.bn_aggr(o[:, b, 6:8], o[:, b, 0:6]), [("bnst", b)], [("bnst", b)])
                    TS(vc[:, b, :], vg32[i2][:], bnst4[:, b, 6:7], None, ALU.subtract, ALU.bypass,
                       [("vg32", i2), ("bnst", b)], [("vc", b)])
                return fn

            def f_p(b):
                def fn():
                    i2 = 0
                    TS(vc[:, b, :], vc[:, b, :], rstd4[:, b:b + 1], None, ALU.mult, ALU.bypass,
                       [("vc", b), "rstd4"], [("vc", b)])
                    bk2 = mm_bank()
                    for h in range(4):
                        sl = slice(l * 512 + h * 128, l * 512 + (h + 1) * 128)
                        MM(PS[bk2][:, h * 128:(h + 1) * 128], vc[:, b, h * 128:(h + 1) * 128], wmb[:, sl], True, True,
                           [("vc", b), "wmb"], [("ps", bk2)])
                    for h in range(4):
                        sl = slice(l * 512 + h * 128, l * 512 + (h + 1) * 128)
                        STT(mix[i2][:, h * 128:(h + 1) * 128], PS[bk2][:, h * 128:(h + 1) * 128], hcol(l, 0, h), Cc[:, sl],
                            ALU.mult, ALU.add, [("ps", bk2), "hv", "Cc"], [("mix", 0)])
                    TT(gpre[:, :, b * 128:(b + 1) * 128], mix[i2][:].rearrange("p (h i) -> p h i", h=4),
                       big[:, U0:U0 + 4, b * 128:(b + 1) * 128], ALU.mult,
                       [("mix", 0)] + [("big", U0 + h) for h in range(4)], [("gpre", b)])
                return fn

            for uc in range(4):
                f_u(uc)()
            for b in range(NB):
                f_v(b)()
            for gc in range(4):
                f_g(gc)()
            ACT(rstd4[:], bnst4[:, :, 7], AF.Ln, [("bnst", b) for b in range(NB)] + ["epsc"], ["rstd4"], bias=epsc[:, 0:1])
            ACT(rstd4[:], rstd4[:], AF.Exp, ["rstd4"], ["rstd4"], scale=-0.5)
            fillers = [f_p(b) for b in range(NB)]

            def gla_block(b):
                i2 = b % 2
                bsl = slice(b * 128, (b + 1) * 128)
                MM(PS[4][:, 0:256], zlr[0:17, bsl], gwb[0:17, l * 256:(l + 1) * 256], True, True,
                   ["zlr", "gwb"], [("ps", 4)])
                ACT(e1[i2][:], PS[4][:, 0:256], AF.Exp, [("ps", 4)], [("e1", 0)], scale=-1.0)
                ACT(sp[i2][:], e1[i2][:], AF.Ln, [("e1", 0)], [("sp", 0)], bias=1.0)
                VCOPY(sphi[:], sp[i2][:], [("sp", 0)], ["sphi"])
                TT(splo[:], sp[i2][:], sphi[:], ALU.subtract, [("sp", 0), "sphi"], ["splo"])
                yield
                bkc = mm_bank()
                for dc in range(2):
                    o = PS[bkc][:, dc * 128:(dc + 1) * 128]
                    MM(o, sphi[:, dc * 128:(dc + 1) * 128], triNb[:], True, False, ["sphi", "triNb"], [("ps", bkc)])
                    MM(o, splo[:, dc * 128:(dc + 1) * 128], triNb[:], False, True, ["splo", "triNb"], [("ps", bkc)])
                VCOPY(cumT[i2][:], PS[bkc][:, 0:256], [("ps", bkc)], [("cumT", i2)])
                ACT(E1[i2][:], cumT[i2][:], AF.Exp, [("cumT", i2)], [("E1", i2)])
                ACT(E2[i2][:], cumT[i2][:], AF.Exp, [("cumT", i2)], [("E2", 0)], scale=-1.0)
                for dc in range(2):
                    ACT(E3[i2][:, dc * 128:(dc + 1) * 128], cumT[i2][:, dc * 128:(dc + 1) * 128], AF.Exp,
                        [("cumT", i2)], [("E3", 0)], scale=-1.0, bias=cumT[i2][:, dc * 128 + 127:dc * 128 + 128])
                r3 = "p (c i) -> p c i"
                STT(qeT[i2][:].rearrange(r3, c=2), qT[:, :, bsl], 0.125, E1[i2][:].rearrange(r3, c=2), ALU.mult, ALU.mult,
                    [("qT", 0), ("qT", 1), ("E1", i2)], [("qeT", i2)])
                TT(keT[i2][:].rearrange(r3, c=2), kT[:, :, bsl], E2[i2][:].rearrange(r3, c=2), ALU.mult,
                   [("kT", 0), ("kT", 1), ("E2", 0)], [("keT", i2)])
                TT(kdT[i2][:].rearrange(r3, c=2), kT[:, :, bsl], E3[i2][:].rearrange(r3, c=2), ALU.mult,
                   [("kT", 0), ("kT", 1), ("E3", 0)], [("kdT", i2)])
                yield
                for dc in range(2):
                    TR(PSB[:, dc * 128:(dc + 1) * 128], kdT[i2][:, dc * 128:(dc + 1) * 128], identb[:],
                       [("kdT", i2), "identb"], [("psb", 0)])
                VCOPY(kd[i2][:], PSB[:, 0:256], [("psb", 0)], [("kd", i2)])
                bks = (5, 4)
                for h in range(4):
                    dc, par = h // 2, h % 2
                    rows = slice(par * 64, par * 64 + 64)
                    MM(PS[bks[par]][:, h * 128:(h + 1) * 128], keT[i2][rows, dc * 128:(dc + 1) * 128],
                       qeT[i2][rows, dc * 128:(dc + 1) * 128], True, True,
                       [("keT", i2), ("qeT", i2)], [("ps", bks[par])])
                for h in range(4):
                    par = h % 2
                    TT(scm[i2][:, h * 128:(h + 1) * 128], PS[bks[par]][:, h * 128:(h + 1) * 128], tri01, ALU.mult,
                       [("ps", bks[par]), "cst"], [("scm", i2)])
                yield
                for h in range(4):
                    dc, par = h // 2, h % 2
                    rows = slice(par * 64, par * 64 + 64)
                    o = PS[6][:, h * 128:(h + 1) * 128]
                    MM(o, big[:, VG0 + b, h * 128:(h + 1) * 128], scm[i2][:, h * 128:(h + 1) * 128], True, False,
                       [("big", VG0 + b), ("scm", i2)], [("ps", 6)])
                    MM(o, Sb[rows, l, dc * 128:(dc + 1) * 128], qeT[i2][rows, dc * 128:(dc + 1) * 128], False, True,
                       [("Sb", l), ("qeT", i2)], [("ps", 6)])
                VCOPY(apre[:, :, bsl], PS[6][:].rearrange("p (h i) -> p h i", h=4), [("ps", 6)], [("apre", b)])
                bk = mm_bank()
                for h in range(4):
                    dc = h // 2
                    MM(PS[bk][:, h * 128:(h + 1) * 128], kd[i2][:, dc * 128:(dc + 1) * 128],
                       big[:, VG0 + b, h * 128:(h + 1) * 128], True, True,
                       [("kd", i2), ("big", VG0 + b)], [("ps", bk)])
                for h in range(4):
                    dc, par = h // 2, h % 2
                    rows = slice(par * 64, par * 64 + 64)
                    STT(S32[rows, l, dc * 128:(dc + 1) * 128], S32[rows, l, dc * 128:(dc + 1) * 128],
                        E1[i2][rows, dc * 128 + 127:dc * 128 + 128], PS[bk][rows, h * 128:(h + 1) * 128],
                        ALU.mult, ALU.add, [("S32", l), ("E1", i2), ("ps", bk)], [("S32", l)])
                ACT(Sb[:, l, :], S32[:, l, :], AF.Copy, [("S32", l)], [("Sb", l)])
                yield

            fi = 0
            gens = [gla_block(b) for b in range(NB)]
            for kstep in range(NB + 3):
                for b in range(NB):
                    if 0 <= kstep - b < 4:
                        next(gens[b])
                        if fi < len(fillers):
                            fillers[fi]()
                            fi += 1
            while fi < len(fillers):
                fillers[fi]()
                fi += 1

            for h in range(4):
                ACT(sq[:, h, :], gpre[:, h, :], AF.Square, [("gpre", b) for b in range(NB)], [("sq", h)])
            for h in range(4):
                ACT(sq[:, 4 + h, :], apre[:, h, :], AF.Square, [("apre", b) for b in range(NB)], [("sq", 4 + h)])
            stats_rstd(lambda c: sq[:, c, :], 4, ones512, "ones512", [("sq", c) for c in range(4)], rstd, "rstd")
            for h in range(4):
                STT(big[:, CAT0 + h, :], gpre[:, h, :], hcol(l, 1, h), rstd[:], ALU.mult, ALU.mult,
                    [("gpre", b) for b in range(NB)] + ["rstd", "hv"], [("big", CAT0 + h)])
            for h in range(4):
                MM(PS[3][:], ones128[:], sq[:, 4 + h, :], True, True, [("sq", 4 + h), "ones128"], [("ps", 3)])
                ACT(rstd2[:], PS[3][:], AF.Ln, [("ps", 3), "epsc"], ["rstd2"], bias=epsc[:, 0:1])
                ACT(rstd2[:], rstd2[:], AF.Exp, ["rstd2"], ["rstd2"], scale=-0.5)
                STT(apre[:, h, :], apre[:, h, :], hcol(l, 2, h), rstd2[:], ALU.mult, ALU.mult,
                    [("apre", b) for b in range(NB)] + ["rstd2", "hv"], [("apre", b) for b in range(NB)])
                TT(big[:, CAT0 + 4 + h, :], apre[:, h, :], big[:, G0 + h, :], ALU.mult,
                   [("apre", b) for b in range(NB)] + [("big", G0 + h)], [("big", CAT0 + 4 + h)])

            k = next_slice()
            W = wsl[k][:, 0:8192].rearrange("p (c n) -> p c n", c=8)
            wkey = ("wslot", k)
            for dc in range(8):
                bk = mm_bank()
                for mc in range(8):
                    MM(PS[bk][:], W[:, mc, dc * 128:(dc + 1) * 128], big[:, CAT0 + mc, :], mc == 0, mc == 7,
                       [wkey, ("big", CAT0 + mc)], [("ps", bk)])
                evac_m(bk, dc)
            post_norm_residual(l, 1)

            if STOP <= 5:
                return
            pre_norm(l, 2)
            for s in range(4):
                k = next_slice()
                W = wsl[k][:, 0:8192].rearrange("p (c n) -> p c n", c=8)
                wkey = ("wslot", k)
                for fc in range(8):
                    ffc = s * 8 + fc
                    bk = mm_bank()
                    for kc in range(8):
                        MM(PS[bk][:], W[:, kc, fc * 128:(fc + 1) * 128], yT[:, kc, :], kc == 0, kc == 7,
                           [wkey, ("yT", kc)], [("ps", bk)])
                    r2 = ffc % 2
                    ACT(vg32[r2][:], PS[bk][:], AF.Relu, [("ps", bk)], [("vg32", r2)])
                    TT(big[:, ffc, :], vg32[r2][:], vg32[r2][:], ALU.mult, [("vg32", r2)], [("big", ffc)])
            for s in range(4):
                k = next_slice()
                W = wsl[k][:, 0:8192].rearrange("p (c n) -> p c n", c=32)
                wkey = ("wslot", k)
                for dl in range(2):
                    dc = s * 2 + dl
                    bk = mm_bank()
                    for ffc in range(32):
                        MM(PS[bk][:], W[:, ffc, dl * 128:(dl + 1) * 128], big[:, ffc, :], ffc == 0, ffc == 31,
                           [wkey, ("big", ffc)], [("ps", bk)])
                    evac_m(bk, dc)
            post_norm_residual(l, 3)

        stage = mflat.rearrange("p (b d) -> p b d", b=NB)
        for t in range(n_tiles):
            DMA("sync", stage, x_d[t * T:(t + 1) * T, :].rearrange("(b p) d -> p b d", p=128), "xin",
                [], [("mT", c) for c in range(8)])
            for c in range(8 if 'notr' not in DBG else 0):
                bk = mm_bank()
                for b in range(NB):
                    TR(PS[bk][:, b * 128:(b + 1) * 128], stage[:, b, c * 128:(c + 1) * 128], ident32,
                       [("mT", 2 * b), ("mT", 2 * b + 1), "cst"], [("ps", bk)])
                ACT(hT[:, c, :], PS[bk][:], AF.Copy, [("ps", bk)], [("hT", c)])
            for l in range(L):
                layer(t, l)
            for b in range(NB if 'notr' not in DBG else 0):
                for half in range(2):
                    bk = mm_bank()
                    for cc in range(4):
                        c = half * 4 + cc
                        TR(PS[bk][:, cc * 128:(cc + 1) * 128], hT[:, c, b * 128:(b + 1) * 128], ident32,
                           [("hT", c), "cst"], [("ps", bk)])
                    ACT(stage[:, b, half * 512:(half + 1) * 512], PS[bk][:], AF.Copy, [("ps", bk)], [("mT", 2 * b + half)])
            DMA("sync", y_d[t * T:(t + 1) * T, :].rearrange("(b p) d -> p b d", p=128), stage, "yout",
                [("mT", c) for c in range(8)], [("yout", t)])

        block = stack.enter_context(nc.Block())
        rec.emit(nc, block, stack)
    return nc


def _consts():
    c = np.zeros((128, 512), np.float32)
    j = np.arange(128)[:, None]
    i = np.arange(128)[None, :]
    c[:, 0:128] = np.eye(128, dtype=np.float32)
    c[:, 128:256] = np.where(j <= i, -1.0 / 16.0, 0.0)
    c[:, 256:384] = np.where(j <= i, 1.0, 0.0)
    c[:, 384:512] = np.where((i // 64) >= (j // 64), 1.0, 0.0)
    return c


def _prep_params(L, pre_mix_g, post_mix_g, pre_ff_g, post_ff_g, gmlp_ln_g, gmlp_ln_b, gmlp_ws, gmlp_bs,
                 gmlp_out_g, gla_gate_w, gla_gate_b, gla_out_g):
    f = np.float32
    gv = np.stack([pre_mix_g, post_mix_g, pre_ff_g, post_ff_g], axis=1).astype(f)
    gv = gv.reshape(L, 4, 8, 128).transpose(3, 0, 1, 2).reshape(128, L * 32)
    hv = np.stack([gmlp_ln_g, gmlp_out_g, gla_out_g], axis=1).astype(f)
    hv = hv.reshape(L, 3, 4, 128).transpose(3, 0, 1, 2).reshape(128, L * 12)
    lnb = np.broadcast_to(np.asarray(gmlp_ln_b, f).reshape(1, L * 512), (128, L * 512))
    bs = np.asarray(gmlp_bs, f).reshape(1, L * 512)
    wsT = np.asarray(gmlp_ws, f).transpose(3, 0, 1, 2).reshape(128, L * 512)
    gwb = np.concatenate([np.asarray(gla_gate_w, f), np.asarray(gla_gate_b, f)[:, None, :]], axis=1)
    gwb = gwb.transpose(1, 0, 2).reshape(17, L * 256)
    return dict(gv=np.ascontiguousarray(gv), hv=np.ascontiguousarray(hv), lnb_bc=np.ascontiguousarray(lnb),
                bs_row=np.ascontiguousarray(bs), wsT=np.ascontiguousarray(wsT), gwb=np.ascontiguousarray(gwb),
                consts=_consts())


def run_model(x, pre_mix_g, w_in, gmlp_ln_g, gmlp_ln_b, gmlp_ws, gmlp_bs, gmlp_out_g,
              gla_gate_w, gla_gate_b, gla_out_g, w_out, post_mix_g, pre_ff_g,
              w_ff1, w_ff2, post_ff_g, n_cores=8, strict_same=True):
    x = np.asarray(x, np.float32)
    B, S, _ = x.shape
    L = int(np.asarray(w_in).shape[0])
    n_tiles = S // T
    nc = build_program(n_tiles, L, strict_same=strict_same)
    params = _prep_params(L, np.asarray(pre_mix_g), np.asarray(post_mix_g), np.asarray(pre_ff_g), np.asarray(post_ff_g),
                          np.asarray(gmlp_ln_g), np.asarray(gmlp_ln_b), np.asarray(gmlp_ws), np.asarray(gmlp_bs),
                          np.asarray(gmlp_out_g), np.asarray(gla_gate_w), np.asarray(gla_gate_b), np.asarray(gla_out_g))
    big = dict(w_in=np.ascontiguousarray(w_in, dtype=np.float32), w_out=np.ascontiguousarray(w_out, dtype=np.float32),
               w_ff1=np.ascontiguousarray(w_ff1, dtype=np.float32), w_ff2=np.ascontiguousarray(w_ff2, dtype=np.float32))
    if n_cores == 8 and B == 4:
        owner = {0: 0, 1: 1, 4: 2, 5: 3}
    else:
        owner = {c: c for c in range(min(B, n_cores))}
    zeros = None
    in_maps = []
    for c in range(n_cores):
        if c in owner:
            m = dict(params)
            m.update(big)
            m["x"] = np.ascontiguousarray(x[owner[c]])
        else:
            if zeros is None:
                zeros = {k: np.zeros_like(v) for k, v in {**params, **big}.items()}
                zeros["x"] = np.zeros_like(x[0])
            m = dict(zeros)
        in_maps.append(m)
    res = run_bass_kernel_spmd(nc, in_maps, core_ids=list(range(n_cores)))
    inv = {b: c for c, b in owner.items()}
    out = np.stack([np.asarray(res.results[inv[b]]["y"], dtype=np.float32) for b in range(B)], axis=0)
    return out


def kernel(**inputs):
    return run_model(**inputs)
```

```python
from contextlib import ExitStack

import os
import numpy as np
import concourse.bass as bass
import concourse.mybir as mybir
from concourse.bass_utils import run_bass_kernel_spmd

AF = mybir.ActivationFunctionType
ALU = mybir.AluOpType
F32 = mybir.dt.float32
BF16 = mybir.dt.bfloat16

D = 1024
DIN = 2576
DFF = 4096
T = 512
NB = 4
EPS = 1e-6
STOP = int(os.environ.get('KSTOP', '99'))
DBG = os.environ.get('KDBG', '')
GS = int(os.environ.get('KGS', '99'))
NSLOT = 3


class Op:
    __slots__ = ("eng", "fn", "deps", "chan", "grp", "sigval", "sigchan", "waits", "is_dma")


class Rec:
    def __init__(self, strict_same=True):
        self.ops = []
        self.last_w = {}
        self.readers = {}
        self.strict_same = strict_same

    def add(self, eng, fn, r=(), w=(), chan=None, grp=None):
        i = len(self.ops)
        deps = {}
        for k in r:
            j = self.last_w.get(k)
            if j is not None:
                deps[j] = True
        for k in w:
            j = self.last_w.get(k)
            if j is not None and j not in deps:
                deps[j] = False
            for j in self.readers.get(k, {}).values():
                if j not in deps:
                    deps[j] = False
        for k in r:
            rk = ("dma", i) if chan is not None else eng
            self.readers.setdefault(k, {})[rk] = i
        for k in w:
            self.last_w[k] = i
            self.readers[k] = {}
        op = Op()
        op.eng = eng
        op.fn = fn
        op.chan = chan
        op.is_dma = chan is not None
        op.grp = grp
        op.deps = [(j, raw) for j, raw in deps.items() if j != i]
        op.sigval = None
        op.sigchan = None
        op.waits = []
        self.ops.append(op)
        return i

    def _needs_sync(self, pj, pi, raw):
        if pj.grp is not None and pj.grp == pi.grp:
            return False
        if pj.is_dma:
            return True
        if pj.eng != pi.eng:
            return True
        if pi.is_dma:
            return True
        if pj.eng == "tensor":
            return False
        return bool(raw and self.strict_same)

    def emit(self, nc, block, stack):
        ops = self.ops
        need_sig = [False] * len(ops)
        for op in ops:
            for j, raw in op.deps:
                if self._needs_sync(ops[j], op, raw):
                    need_sig[j] = True
                    op.waits.append(j)
        cnt = {}
        for i, op in enumerate(ops):
            if op.is_dma:
                cnt[op.chan] = cnt.get(op.chan, 0) + 16
                op.sigchan, op.sigval = op.chan, cnt[op.chan]
            elif need_sig[i]:
                cnt[op.eng] = cnt.get(op.eng, 0) + 1
                op.sigchan, op.sigval = op.eng, cnt[op.eng]
        sems = {}
        for ch in cnt:
            nm = ch if isinstance(ch, str) else "_".join(str(c) for c in ch)
            sems[ch] = stack.enter_context(nc.semaphore("s_" + nm))
        dma_final = {ch: v for ch, v in cnt.items() if ch not in ("tensor", "vector", "scalar", "gpsimd", "sync")}

        def run(eng_name):
            def body(e):
                waited = {}
                for op in ops:
                    if op.eng != eng_name:
                        continue
                    req = {}
                    for j in op.waits:
                        ch, v = ops[j].sigchan, ops[j].sigval
                        if v > req.get(ch, 0):
                            req[ch] = v
                    for ch, v in req.items():
                        if waited.get(ch, 0) >= v:
                            continue
                        e.wait_ge(sems[ch], v)
                        waited[ch] = v
                    ins = op.fn(e)
                    if op.sigval is not None:
                        ins.then_inc(sems[op.sigchan], 16 if op.is_dma else 1)
                if eng_name == "sync":
                    for ch, v in dma_final.items():
                        if waited.get(ch, 0) < v:
                            e.wait_ge(sems[ch], v)
            return body

        block.sync(run("sync"))
        block.gpsimd(run("gpsimd"))
        block.tensor(run("tensor"))
        block.vector(run("vector"))
        block.scalar(run("scalar"))


def build_program(n_tiles, n_layers, strict_same=True):
    L = n_layers
    S = n_tiles * T
    nc = bass.Bass("TRN2", target_bir_lowering=False)
    dt = nc.dram_tensor
    x_d = dt("x", [S, D], F32, kind="ExternalInput").ap()
    y_d = dt("y", [S, D], F32, kind="ExternalOutput").ap()
    win_d = dt("w_in", [L, D, DIN], F32, kind="ExternalInput").ap()
    wout_d = dt("w_out", [L, D, D], F32, kind="ExternalInput").ap()
    w1_d = dt("w_ff1", [L, D, DFF], F32, kind="ExternalInput").ap()
    w2_d = dt("w_ff2", [L, DFF, D], F32, kind="ExternalInput").ap()
    gv_d = dt("gv", [128, L * 4 * 8], F32, kind="ExternalInput").ap()
    hv_d = dt("hv", [128, L * 3 * 4], F32, kind="ExternalInput").ap()
    lnb_d = dt("lnb_bc", [128, L * 512], F32, kind="ExternalInput").ap()
    bs_d = dt("bs_row", [1, L * 512], F32, kind="ExternalInput").ap()
    wsT_d = dt("wsT", [128, L * 512], F32, kind="ExternalInput").ap()
    gwb_d = dt("gwb", [17, L * 256], F32, kind="ExternalInput").ap()
    cst_d = dt("consts", [128, 512], F32, kind="ExternalInput").ap()

    rec = Rec(strict_same=strict_same)
    stack = ExitStack()
    with stack:
        def sb(name, shape, dtype):
            return stack.enter_context(nc.sbuf_tensor(name, shape, dtype))

        def psum(name, shape, dtype):
            return stack.enter_context(nc.psum_tensor(name, shape, dtype))

        hT = sb("hT", [128, 8, T], F32)
        yT = sb("yT", [128, 8, T], BF16)
        sq = sb("sq", [128, 8, T], BF16)
        mT = sb("mT", [128, 8, T], F32)
        big = sb("big", [128, 32, T], BF16)
        wsl = [sb(f"wslot{k}", [128, 8192], BF16) for k in range(NSLOT)]
        rstd = sb("rstd", [128, T], F32)
        rstd2 = sb("rstd2", [128, T], F32)
        apre = sb("apre", [128, 4, T], F32)
        gpre = sb("gpre", [128, 4, T], F32)
        vg32 = [sb(f"vg32_{i}", [128, 512], F32) for i in range(2)]
        vc = sb("vc", [128, NB, 512], BF16)
        bnst4 = sb("bnst4", [128, NB, 8], F32)
        rstd4 = sb("rstd4", [128, NB], F32)
        mix = [sb("mix_0", [128, 512], F32)] * 2
        e1 = [sb("e1_0", [128, 256], F32)] * 2
        sp = [sb("sp_0", [128, 256], F32)] * 2
        cumT = [sb(f"cumT_{i}", [128, 256], F32) for i in range(2)]
        E1 = [sb(f"E1_{i}", [128, 256], F32) for i in range(2)]
        E2 = [sb("E2_0", [128, 256], F32)] * 2
        E3 = [sb("E3_0", [128, 256], F32)] * 2
        qeT = [sb(f"qeT_{i}", [128, 256], BF16) for i in range(2)]
        keT = [sb(f"keT_{i}", [128, 256], BF16) for i in range(2)]
        kdT = [sb(f"kdT_{i}", [128, 256], BF16) for i in range(2)]
        kd = [sb(f"kd_{i}", [128, 256], BF16) for i in range(2)]
        scm = [sb(f"scm_{i}", [128, 512], BF16) for i in range(2)]
        zlr = sb("zlr", [32, T], BF16)
        S32 = sb("S32", [128, L, 256], F32)
        Sb = sb("Sb", [128, L, 256], BF16)
        gv = sb("gv_s", [128, L * 32], F32)
        hv = sb("hv_s", [128, L * 12], F32)
        assert L <= 4
        mflat = mT[:].rearrange("p c t -> p (c t)")
        lnb = mflat[:, 0:L * 512]
        wm32 = mflat[:, 2048:2048 + L * 512]
        bsr = apre[:].rearrange("p c t -> p (c t)")[0:1, 0:L * 512]
        wmb = sb("wmb", [128, L * 512], BF16)
        Cc = sb("Cc", [128, L * 512], F32)
        gwb = sb("gwb_s", [17, L * 256], BF16)
        cst = sb("cst", [128, 512], F32)
        identb = sb("identb", [128, 128], BF16)
        triNb = sb("triNb", [128, 128], BF16)
        tri2b = sb("tri2b", [128, 256], BF16)
        sphi = sb("sphi", [128, 256], BF16)
        splo = sb("splo", [128, 256], BF16)
        onesD = sb("onesD", [128, 128], BF16)
        ones512 = sb("ones512", [128, 128], BF16)
        ones128 = sb("ones128", [128, 128], BF16)
        onerow = sb("onerow", [1, 128], F32)
        epsc = sb("epsc", [128, 1], F32)

        PS = [psum(f"ps{i}", [128, 512], F32) for i in range(7)]
        PSB = psum("psb", [128, 1024], BF16)

        ident32 = cst[:, 0:128]
        triN = cst[:, 128:256]
        tri01 = cst[:, 256:384]
        gmask = cst[:, 384:512]

        mmrot = [0]

        def mm_bank():
            b = mmrot[0] % 3
            mmrot[0] += 1
            return b

        def MM(out, lhsT, rhs, start, stop, r, w):
            rec.add("tensor", lambda e, o=out, l=lhsT, rr=rhs, s=start, p=stop: e.matmul(o, l, rr, start=s, stop=p), r=r, w=w)

        def TR(out, in_, ident, r, w):
            rec.add("tensor", lambda e, o=out, i=in_, d=ident: e.transpose(o, i, d), r=r, w=w)

        def ACT(out, in_, func, r, w, bias=None, scale=None):
            def fn(e, o=out, i=in_, f=func, b=bias, s=scale):
                kw = {}
                if b is not None:
                    kw["bias"] = b
                if s is not None:
                    kw["scale"] = s
                return e.activation(o, i, f, **kw)
            rec.add("scalar", fn, r=r, w=w)

        def V(fn, r, w):
            rec.add("vector", fn, r=r, w=w)

        def STT(out, in0, scalar, in1, op0, op1, r, w):
            V(lambda e, o=out, a=in0, s=scalar, b=in1, p0=op0, p1=op1: e.scalar_tensor_tensor(o, a, s, b, p0, p1), r, w)

        def TT(out, in0, in1, op, r, w):
            V(lambda e, o=out, a=in0, b=in1, p=op: e.tensor_tensor(o, a, b, p), r, w)

        def TS(out, in0, s1, s2, op0, op1, r, w):
            V(lambda e, o=out, a=in0, x1=s1, x2=s2, p0=op0, p1=op1: e.tensor_scalar(o, a, x1, x2, p0, p1), r, w)

        def VCOPY(out, in_, r, w):
            V(lambda e, o=out, i=in_: e.tensor_copy(o, i), r, w)

        def DMA(eng, out, in_, chan, r, w, grp=None):
            rec.add(eng, lambda e, o=out, i=in_: e.dma_start(out=o, in_=i), r=r, w=w, chan=chan, grp=grp)

        MK = [("mT", c) for c in range(8)]
        AK = [("apre", b) for b in range(NB)]
        DMA("sync", cst[:], cst_d[:, :], "ld0", [], ["cst"])
        DMA("sync", gv[:], gv_d[:, :], "ld1", [], ["gv"])
        DMA("sync", hv[:], hv_d[:, :], "ld2", [], ["hv"])
        DMA("sync", lnb, lnb_d[:, :], "ld3", [], MK)
        DMA("sync", bsr, bs_d[:, :], "ld4", [], AK)
        DMA("sync", wm32, wsT_d[:, :], "ld5", MK, MK)
        DMA("gpsimd", gwb[:], gwb_d[:, :], "ld6", [], ["gwb"])
        V(lambda e: e.memset(onesD[:], 1.0 / 1024.0), [], ["onesD"])
        V(lambda e: e.memset(ones512[:], 1.0 / 512.0), [], ["ones512"])
        V(lambda e: e.memset(ones128[:], 1.0 / 128.0), [], ["ones128"])
        V(lambda e: e.memset(onerow[:], 1.0), [], ["onerow"])
        V(lambda e: e.memset(epsc[:], EPS), [], ["epsc"])
        V(lambda e: e.memset(zlr[:], 1.0), [], ["zlr"])
        V(lambda e: e.memset(S32[:], 0.0), [], [("S32", l) for l in range(L)])
        V(lambda e: e.memset(Sb[:], 0.0), [], [("Sb", l) for l in range(L)])
        VCOPY(identb[:], ident32, ["cst"], ["identb"])
        VCOPY(triNb[:], triN, ["cst"], ["triNb"])
        VCOPY(tri2b[:, 0:128], tri01, ["cst"], ["tri2b"])
        VCOPY(tri2b[:, 128:256], tri01, ["cst", "tri2b"], ["tri2b"])
        for l in range(L):
            for h in range(4):
                sl = slice(l * 512 + h * 128, l * 512 + (h + 1) * 128)
                TT(wm32[:, sl], wm32[:, sl], gmask, ALU.mult, MK + ["cst"], MK)
        VCOPY(wmb[:], wm32, MK, ["wmb"])
        for l in range(L if 'nocmm' not in DBG else 0):
            bk = mm_bank()
            for h in range(4):
                sl = slice(l * 512 + h * 128, l * 512 + (h + 1) * 128)
                o = PS[bk][:, h * 128:(h + 1) * 128]
                if 'c_b' not in DBG:
                    MM(o, lnb[:, sl], wm32[:, sl], True, 'c_a' in DBG, MK, [("ps", bk)])
                if 'c_a' not in DBG:
                    MM(o, onerow[0:1, :], bsr[0:1, sl], 'c_b' in DBG, True, ["onerow"] + AK, [("ps", bk)])
            if 'c_nocopy' not in DBG:
                VCOPY(Cc[:, l * 512:(l + 1) * 512], PS[bk][:], [("ps", bk)], ["Cc"])

        slices = []
        for t in range(n_tiles):
            for l in range(L):
                slices += [(l, "in", 1), (l, "in", 2), (l, "in", 0), (l, "out", 0)]
                slices += [(l, "w1", i) for i in range(4)]
                slices += [(l, "w2", i) for i in range(4)]
        issued = [0]

        def issue_load(n):
            l, kind, i = slices[n]
            k = n % NSLOT
            slot = wsl[k]
            key = ("wslot", k)
            if kind == "in":
                c0 = i * 1024
                c1 = min(DIN, c0 + 1024)
                w = c1 - c0
                src = win_d[l].rearrange("(c p) n -> p c n", p=128)[:, :, c0:c1]
                dst = slot[:, 0:8 * w].rearrange("p (c n) -> p c n", c=8)
                DMA("gpsimd", dst, src, ("w", k), [], [key])
            elif kind == "out":
                src = wout_d[l].rearrange("(c p) n -> p c n", p=128)
                dst = slot[:, 0:8192].rearrange("p (c n) -> p c n", c=8)
                DMA("gpsimd", dst, src, ("w", k), [], [key])
            elif kind == "w1":
                src = w1_d[l].rearrange("(c p) n -> p c n", p=128)[:, :, i * 1024:(i + 1) * 1024]
                dst = slot[:, 0:8192].rearrange("p (c n) -> p c n", c=8)
                DMA("gpsimd", dst, src, ("w", k), [], [key])
            else:
                srcall = w2_d[l].rearrange("(c p) n -> p c n", p=128)[:, :, i * 256:(i + 1) * 256]
                dstall = slot[:, 0:8192].rearrange("p (c n) -> p c n", c=32)
                for q in range(4):
                    DMA("gpsimd", dstall[:, q * 8:(q + 1) * 8, :], srcall[:, q * 8:(q + 1) * 8, :], ("w", k), [], [key],
                        grp=("wfill", n))

        cur = [0]

        def next_slice(hold=0):
            n = cur[0]
            cur[0] += 1
            while issued[0] < min(len(slices), n + NSLOT - hold):
                issue_load(issued[0])
                issued[0] += 1
            return n % NSLOT

        def gcol(l, k, c):
            j = (l * 4 + k) * 8 + c
            return gv[:, j:j + 1]

        def hcol(l, k, h):
            j = (l * 3 + k) * 4 + h
            return hv[:, j:j + 1]

        def stats_rstd(sq_ap_fn, nchunk, ones_t, ones_key, sq_keys, out_rstd, out_key):
            for c in range(nchunk):
                MM(PS[3][:], ones_t[:], sq_ap_fn(c), c == 0, c == nchunk - 1, [sq_keys[c], ones_key], [("ps", 3)])
            if 'sqrt' in DBG:
                ACT(out_rstd[:], PS[3][:], AF.Sqrt, [("ps", 3), "epsc"], [out_key], bias=epsc[:, 0:1])
                V(lambda e, o=out_rstd: e.reciprocal(o[:], o[:]), [out_key], [out_key])
                return
            ACT(out_rstd[:], PS[3][:], AF.Ln, [("ps", 3), "epsc"], [out_key], bias=epsc[:, 0:1])
            ACT(out_rstd[:], out_rstd[:], AF.Exp, [out_key], [out_key], scale=-0.5)

        def pre_norm(l, k):
            for c in range(8):
                ACT(sq[:, c, :], hT[:, c, :], AF.Square, [("hT", c)], [("sq", c)])
            stats_rstd(lambda c: sq[:, c, :], 8, onesD, "onesD", [("sq", c) for c in range(8)], rstd, "rstd")
            for c in range(8):
                STT(yT[:, c, :], hT[:, c, :], gcol(l, k, c), rstd[:], ALU.mult, ALU.mult,
                    [("hT", c), "rstd", "gv"], [("yT", c)])

        def post_norm_residual(l, k):
            stats_rstd(lambda c: sq[:, c, :], 8, onesD, "onesD", [("sq", c) for c in range(8)], rstd, "rstd")
            for c in range(8):
                STT(mT[:, c, :], mT[:, c, :], gcol(l, k, c), rstd[:], ALU.mult, ALU.mult,
                    [("mT", c), "rstd", "gv"], [("mT", c)])
                TT(hT[:, c, :], hT[:, c, :], mT[:, c, :], ALU.add, [("hT", c), ("mT", c)], [("hT", c)])

        def evac_m(bk, c):
            ACT(mT[:, c, :], PS[bk][:], AF.Copy, [("ps", bk)], [("mT", c)])
            ACT(sq[:, c, :], PS[bk][:], AF.Square, [("ps", bk)], [("sq", c)])

        U0, G0, CAT0, VG0, Q0, K0 = 0, 4, 8, 16, 20, 24

        qT = sb("qT", [128, 2, T], F32)
        kT = sb("kT", [128, 2, T], F32)

        def layer(t, l):
            pre_norm(l, 0)
            k = next_slice()
            W = wsl[k][:, 0:8192].rearrange("p (c n) -> p c n", c=8)
            wkey = ("wslot", k)
            for qc in range(2):
                bk = mm_bank()
                for kc in range(8):
                    MM(PS[bk][:], W[:, kc, qc * 128:(qc + 1) * 128], yT[:, kc, :], kc == 0, kc == 7,
                       [wkey, ("yT", kc)], [("ps", bk)])
                VCOPY(qT[:, qc, :], PS[bk][:], [("ps", bk)], [("qT", qc)])
            for qc in range(2):
                bk = mm_bank()
                for kc in range(8):
                    MM(PS[bk][:], W[:, kc, 256 + qc * 128:256 + (qc + 1) * 128], yT[:, kc, :], kc == 0, kc == 7,
                       [wkey, ("yT", kc)], [("ps", bk)])
                VCOPY(kT[:, qc, :], PS[bk][:], [("ps", bk)], [("kT", qc)])
            for b in range(NB):
                bk = mm_bank()
                for kc in range(8):
                    MM(PS[bk][:], yT[:, kc, b * 128:(b + 1) * 128], W[:, kc, 512:1024], kc == 0, kc == 7,
                       [wkey, ("yT", kc)], [("ps", bk)])
                VCOPY(big[:, VG0 + b, :], PS[bk][:], [("ps", bk)], [("big", VG0 + b)])
            kB = next_slice()
            WB = wsl[kB][:, 0:8 * 528].rearrange("p (c n) -> p c n", c=8)
            wkeyB = ("wslot", kB)
            bk = mm_bank()
            for kc in range(8):
                MM(PS[bk][0:16, :], WB[:, kc, 512:528], yT[:, kc, :], kc == 0, kc == 7,
                   [wkeyB, ("yT", kc)], [("ps", bk)])
            VCOPY(zlr[0:16, :], PS[bk][0:16, :], [("ps", bk)], ["zlr"])
            kC = next_slice(hold=1)
            WC = wsl[kC][:, 0:8192].rearrange("p (c n) -> p c n", c=8)
            wkeyC = ("wslot", kC)

            def f_g(gc):
                def fn():
                    bk = mm_bank()
                    for kc in range(8):
                        MM(PS[bk][:], WB[:, kc, gc * 128:(gc + 1) * 128], yT[:, kc, :], kc == 0, kc == 7,
                           [wkeyB, ("yT", kc)], [("ps", bk)])
                    ACT(big[:, G0 + gc, :], PS[bk][:], AF.Silu, [("ps", bk)], [("big", G0 + gc)])
                return fn

            def f_u(uc):
                def fn():
                    bk = mm_bank()
                    for kc in range(8):
                        MM(PS[bk][:], WC[:, kc, uc * 128:(uc + 1) * 128], yT[:, kc, :], kc == 0, kc == 7,
                           [wkeyC, ("yT", kc)], [("ps", bk)])
                    ACT(big[:, U0 + uc, :], PS[bk][:], AF.Gelu_apprx_tanh, [("ps", bk)], [("big", U0 + uc)])
                return fn

            def f_v(b):
                def fn():
                    i2 = b % 2
                    bk = mm_bank()
                    for kc in range(8):
                        MM(PS[bk][:], yT[:, kc, b * 128:(b + 1) * 128], WC[:, kc, 512:1024], kc == 0, kc == 7,
                           [wkeyC, ("yT", kc)], [("ps", bk)])
                    ACT(vg32[i2][:], PS[bk][:], AF.Gelu_apprx_tanh, [("ps", bk)], [("vg32", i2)])
                    V(lambda e, o=bnst4, i=vg32[i2]: e.bn_stats(o[:, b, 0:6], i[:]), [("vg32", i2)], [("bnst", b)])
                    V(lambda e, o=bnst4: e.bn_aggr(o[:, b, 6:8], o[:, b, 0:6]), [("bnst", b)], [("bnst", b)])
                    TS(vc[:, b, :], vg32[i2][:], bnst4[:, b, 6:7], None, ALU.subtract, ALU.bypass,
                       [("vg32", i2), ("bnst", b)], [("vc", b)])
                return fn

            def f_p(b):
                def fn():
                    i2 = 0
                    TS(vc[:, b, :], vc[:, b, :], rstd4[:, b:b + 1], None, ALU.mult, ALU.bypass,
                       [("vc", b), "rstd4"], [("vc", b)])
                    bk2 = mm_bank()
                    for h in range(4):
                        sl = slice(l * 512 + h * 128, l * 512 + (h + 1) * 128)
                        MM(PS[bk2][:, h * 128:(h + 1) * 128], vc[:, b, h * 128:(h + 1) * 128], wmb[:, sl], True, True,
                           [("vc", b), "wmb"], [("ps", bk2)])
                    for h in range(4):
                        sl = slice(l * 512 + h * 128, l * 512 + (h + 1) * 128)
                        STT(mix[i2][:, h * 128:(h + 1) * 128], PS[bk2][:, h * 128:(h + 1) * 128], hcol(l, 0, h), Cc[:, sl],
                            ALU.mult, ALU.add, [("ps", bk2), "hv", "Cc"], [("mix", 0)])
                    TT(gpre[:, :, b * 128:(b + 1) * 128], mix[i2][:].rearrange("p (h i) -> p h i", h=4),
                       big[:, U0:U0 + 4, b * 128:(b + 1) * 128], ALU.mult,
                       [("mix", 0)] + [("big", U0 + h) for h in range(4)], [("gpre", b)])
                return fn

            for uc in range(4):
                f_u(uc)()
            for b in range(NB):
                f_v(b)()
            for gc in range(4):
                f_g(gc)()
            ACT(rstd4[:], bnst4[:, :, 7], AF.Ln, [("bnst", b) for b in range(NB)] + ["epsc"], ["rstd4"], bias=epsc[:, 0:1])
            ACT(rstd4[:], rstd4[:], AF.Exp, ["rstd4"], ["rstd4"], scale=-0.5)
            fillers = [f_p(b) for b in range(NB)]

            def gla_block(b):
                i2 = b % 2
                bsl = slice(b * 128, (b + 1) * 128)
                MM(PS[4][:, 0:256], zlr[0:17, bsl], gwb[0:17, l * 256:(l + 1) * 256], True, True,
                   ["zlr", "gwb"], [("ps", 4)])
                ACT(e1[i2][:], PS[4][:, 0:256], AF.Exp, [("ps", 4)], [("e1", 0)], scale=-1.0)
                ACT(sp[i2][:], e1[i2][:], AF.Ln, [("e1", 0)], [("sp", 0)], bias=1.0)
                ACT(sphi[:], e1[i2][:], AF.Ln, [("e1", 0)], ["sphi"], bias=1.0)
                TT(splo[:], sp[i2][:], sphi[:], ALU.subtract, [("sp", 0), "sphi"], ["splo"])
                yield
                bkc = mm_bank()
                for dc in range(2):
                    o = PS[bkc][:, dc * 128:(dc + 1) * 128]
                    MM(o, sphi[:, dc * 128:(dc + 1) * 128], triNb[:], True, False, ["sphi", "triNb"], [("ps", bkc)])
                    MM(o, splo[:, dc * 128:(dc + 1) * 128], triNb[:], False, True, ["splo", "triNb"], [("ps", bkc)])
                ACT(cumT[i2][:], PS[bkc][:, 0:256], AF.Copy, [("ps", bkc)], [("cumT", i2)])
                ACT(E1[i2][:], cumT[i2][:], AF.Exp, [("cumT", i2)], [("E1", i2)])
                ACT(E2[i2][:], cumT[i2][:], AF.Exp, [("cumT", i2)], [("E2", 0)], scale=-1.0)
                for dc in range(2):
                    ACT(E3[i2][:, dc * 128:(dc + 1) * 128], cumT[i2][:, dc * 128:(dc + 1) * 128], AF.Exp,
                        [("cumT", i2)], [("E3", 0)], scale=-1.0, bias=cumT[i2][:, dc * 128 + 127:dc * 128 + 128])
                r3 = "p (c i) -> p c i"
                STT(qeT[i2][:].rearrange(r3, c=2), qT[:, :, bsl], 0.125, E1[i2][:].rearrange(r3, c=2), ALU.mult, ALU.mult,
                    [("qT", 0), ("qT", 1), ("E1", i2)], [("qeT", i2)])
                TT(keT[i2][:].rearrange(r3, c=2), kT[:, :, bsl], E2[i2][:].rearrange(r3, c=2), ALU.mult,
                   [("kT", 0), ("kT", 1), ("E2", 0)], [("keT", i2)])
                TT(kdT[i2][:].rearrange(r3, c=2), kT[:, :, bsl], E3[i2][:].rearrange(r3, c=2), ALU.mult,
                   [("kT", 0), ("kT", 1), ("E3", 0)], [("kdT", i2)])
                yield
                for dc in range(2):
                    TR(PSB[:, dc * 128:(dc + 1) * 128], kdT[i2][:, dc * 128:(dc + 1) * 128], identb[:],
                       [("kdT", i2), "identb"], [("psb", 0)])
                ACT(kd[i2][:], PSB[:, 0:256], AF.Copy, [("psb", 0)], [("kd", i2)])
                bks = (5, 4)
                for h in range(4):
                    dc, par = h // 2, h % 2
                    rows = slice(par * 64, par * 64 + 64)
                    MM(PS[bks[par]][:, dc * 128:(dc + 1) * 128], keT[i2][rows, dc * 128:(dc + 1) * 128],
                       qeT[i2][rows, dc * 128:(dc + 1) * 128], True, True,
                       [("keT", i2), ("qeT", i2)], [("ps", bks[par])])
                for par in range(2):
                    TT(scm[i2][:, par * 256:(par + 1) * 256], PS[bks[par]][:, 0:256], tri2b[:], ALU.mult,
                       [("ps", bks[par]), "tri2b"], [("scm", i2)])
                yield
                for h in range(4):
                    dc, par = h // 2, h % 2
                    rows = slice(par * 64, par * 64 + 64)
                    o = PS[6][:, h * 128:(h + 1) * 128]
                    MM(o, big[:, VG0 + b, h * 128:(h + 1) * 128], scm[i2][:, (par * 2 + dc) * 128:(par * 2 + dc + 1) * 128], True, False,
                       [("big", VG0 + b), ("scm", i2)], [("ps", 6)])
                    MM(o, Sb[rows, l, dc * 128:(dc + 1) * 128], qeT[i2][rows, dc * 128:(dc + 1) * 128], False, True,
                       [("Sb", l), ("qeT", i2)], [("ps", 6)])
                ACT(apre[:, :, bsl], PS[6][:].rearrange("p (h i) -> p h i", h=4), AF.Copy, [("ps", 6)], [("apre", b)])
                bk = mm_bank()
                for h in range(4):
                    dc = h // 2
                    MM(PS[bk][:, h * 128:(h + 1) * 128], kd[i2][:, dc * 128:(dc + 1) * 128],
                       big[:, VG0 + b, h * 128:(h + 1) * 128], True, True,
                       [("kd", i2), ("big", VG0 + b)], [("ps", bk)])
                for h in range(4):
                    dc, par = h // 2, h % 2
                    rows = slice(par * 64, par * 64 + 64)
                    STT(S32[rows, l, dc * 128:(dc + 1) * 128], S32[rows, l, dc * 128:(dc + 1) * 128],
                        E1[i2][rows, dc * 128 + 127:dc * 128 + 128], PS[bk][rows, h * 128:(h + 1) * 128],
                        ALU.mult, ALU.add, [("S32", l), ("E1", i2), ("ps", bk)], [("S32", l)])
                ACT(Sb[:, l, :], S32[:, l, :], AF.Copy, [("S32", l)], [("Sb", l)])
                yield

            fi = 0
            gens = [gla_block(b) for b in range(NB)]
            for kstep in range(NB + 3):
                for b in range(NB):
                    if 0 <= kstep - b < 4:
                        next(gens[b])
                        if fi < len(fillers):
                            fillers[fi]()
                            fi += 1
            while fi < len(fillers):
                fillers[fi]()
                fi += 1

            for h in range(4):
                ACT(sq[:, h, :], gpre[:, h, :], AF.Square, [("gpre", b) for b in range(NB)], [("sq", h)])
            for h in range(4):
                ACT(sq[:, 4 + h, :], apre[:, h, :], AF.Square, [("apre", b) for b in range(NB)], [("sq", 4 + h)])
            stats_rstd(lambda c: sq[:, c, :], 4, ones512, "ones512", [("sq", c) for c in range(4)], rstd, "rstd")
            for h in range(4):
                STT(big[:, CAT0 + h, :], gpre[:, h, :], hcol(l, 1, h), rstd[:], ALU.mult, ALU.mult,
                    [("gpre", b) for b in range(NB)] + ["rstd", "hv"], [("big", CAT0 + h)])
            for h in range(4):
                MM(PS[3][:], ones128[:], sq[:, 4 + h, :], True, True, [("sq", 4 + h), "ones128"], [("ps", 3)])
                ACT(rstd2[:], PS[3][:], AF.Ln, [("ps", 3), "epsc"], ["rstd2"], bias=epsc[:, 0:1])
                ACT(rstd2[:], rstd2[:], AF.Exp, ["rstd2"], ["rstd2"], scale=-0.5)
                STT(apre[:, h, :], apre[:, h, :], hcol(l, 2, h), rstd2[:], ALU.mult, ALU.mult,
                    [("apre", b) for b in range(NB)] + ["rstd2", "hv"], [("apre", b) for b in range(NB)])
                TT(big[:, CAT0 + 4 + h, :], apre[:, h, :], big[:, G0 + h, :], ALU.mult,
                   [("apre", b) for b in range(NB)] + [("big", G0 + h)], [("big", CAT0 + 4 + h)])

            k = next_slice()
            W = wsl[k][:, 0:8192].rearrange("p (c n) -> p c n", c=8)
            wkey = ("wslot", k)
            for dc in range(8):
                bk = mm_bank()
                for mc in range(8):
                    MM(PS[bk][:], W[:, mc, dc * 128:(dc + 1) * 128], big[:, CAT0 + mc, :], mc == 0, mc == 7,
                       [wkey, ("big", CAT0 + mc)], [("ps", bk)])
                evac_m(bk, dc)
            post_norm_residual(l, 1)

            if STOP <= 5:
                return
            pre_norm(l, 2)
            for s in range(4):
                k = next_slice()
                W = wsl[k][:, 0:8192].rearrange("p (c n) -> p c n", c=8)
                wkey = ("wslot", k)
                for fc in range(8):
                    ffc = s * 8 + fc
                    bk = mm_bank()
                    for kc in range(8):
                        MM(PS[bk][:], W[:, kc, fc * 128:(fc + 1) * 128], yT[:, kc, :], kc == 0, kc == 7,
                           [wkey, ("yT", kc)], [("ps", bk)])
                    r2 = ffc % 2
                    ACT(vg32[r2][:], PS[bk][:], AF.Relu, [("ps", bk)], [("vg32", r2)])
                    TT(big[:, ffc, :], vg32[r2][:], vg32[r2][:], ALU.mult, [("vg32", r2)], [("big", ffc)])
            for s in range(4):
                k = next_slice()
                W = wsl[k][:, 0:8192].rearrange("p (c n) -> p c n", c=32)
                wkey = ("wslot", k)
                for dl in range(2):
                    dc = s * 2 + dl
                    bk = mm_bank()
                    for ffc in range(32):
                        MM(PS[bk][:], W[:, ffc, dl * 128:(dl + 1) * 128], big[:, ffc, :], ffc == 0, ffc == 31,
                           [wkey, ("big", ffc)], [("ps", bk)])
                    evac_m(bk, dc)
            post_norm_residual(l, 3)

        stage = mflat.rearrange("p (b d) -> p b d", b=NB)
        for t in range(n_tiles):
            DMA("sync", stage, x_d[t * T:(t + 1) * T, :].rearrange("(b p) d -> p b d", p=128), "xin",
                [], [("mT", c) for c in range(8)])
            for c in range(8 if 'notr' not in DBG else 0):
                bk = mm_bank()
                for b in range(NB):
                    TR(PS[bk][:, b * 128:(b + 1) * 128], stage[:, b, c * 128:(c + 1) * 128], ident32,
                       [("mT", 2 * b), ("mT", 2 * b + 1), "cst"], [("ps", bk)])
                ACT(hT[:, c, :], PS[bk][:], AF.Copy, [("ps", bk)], [("hT", c)])
            for l in range(L):
                layer(t, l)
            for b in range(NB if 'notr' not in DBG else 0):
                for half in range(2):
                    bk = mm_bank()
                    for cc in range(4):
                        c = half * 4 + cc
                        TR(PS[bk][:, cc * 128:(cc + 1) * 128], hT[:, c, b * 128:(b + 1) * 128], ident32,
                           [("hT", c), "cst"], [("ps", bk)])
                    ACT(stage[:, b, half * 512:(half + 1) * 512], PS[bk][:], AF.Copy, [("ps", bk)], [("mT", 2 * b + half)])
            DMA("sync", y_d[t * T:(t + 1) * T, :].rearrange("(b p) d -> p b d", p=128), stage, "yout",
                [("mT", c) for c in range(8)], [("yout", t)])

        block = stack.enter_context(nc.Block())
        rec.emit(nc, block, stack)
    return nc


def _consts():
    c = np.zeros((128, 512), np.float32)
    j = np.arange(128)[:, None]
    i = np.arange(128)[None, :]
    c[:, 0:128] = np.eye(128, dtype=np.float32)
    c[:, 128:256] = np.where(j <= i, -1.0 / 16.0, 0.0)
    c[:, 256:384] = np.where(j <= i, 1.0, 0.0)
    c[:, 384:512] = np.where((i // 64) >= (j // 64), 1.0, 0.0)
    return c


def _prep_params(L, pre_mix_g, post_mix_g, pre_ff_g, post_ff_g, gmlp_ln_g, gmlp_ln_b, gmlp_ws, gmlp_bs,
                 gmlp_out_g, gla_gate_w, gla_gate_b, gla_out_g):
    f = np.float32
    gv = np.stack([pre_mix_g, post_mix_g, pre_ff_g, post_ff_g], axis=1).astype(f)
    gv = gv.reshape(L, 4, 8, 128).transpose(3, 0, 1, 2).reshape(128, L * 32)
    hv = np.stack([gmlp_ln_g, gmlp_out_g, gla_out_g], axis=1).astype(f)
    hv = hv.reshape(L, 3, 4, 128).transpose(3, 0, 1, 2).reshape(128, L * 12)
    lnb = np.broadcast_to(np.asarray(gmlp_ln_b, f).reshape(1, L * 512), (128, L * 512))
    bs = np.asarray(gmlp_bs, f).reshape(1, L * 512)
    wsT = np.asarray(gmlp_ws, f).transpose(3, 0, 1, 2).reshape(128, L * 512)
    gwb = np.concatenate([np.asarray(gla_gate_w, f), np.asarray(gla_gate_b, f)[:, None, :]], axis=1)
    gwb = gwb.transpose(1, 0, 2).reshape(17, L * 256)
    return dict(gv=np.ascontiguousarray(gv), hv=np.ascontiguousarray(hv), lnb_bc=np.ascontiguousarray(lnb),
                bs_row=np.ascontiguousarray(bs), wsT=np.ascontiguousarray(wsT), gwb=np.ascontiguousarray(gwb),
                consts=_consts())


def run_model(x, pre_mix_g, w_in, gmlp_ln_g, gmlp_ln_b, gmlp_ws, gmlp_bs, gmlp_out_g,
              gla_gate_w, gla_gate_b, gla_out_g, w_out, post_mix_g, pre_ff_g,
              w_ff1, w_ff2, post_ff_g, n_cores=8, strict_same=True):
    x = np.asarray(x, np.float32)
    B, S, _ = x.shape
    L = int(np.asarray(w_in).shape[0])
    n_tiles = S // T
    nc = build_program(n_tiles, L, strict_same=strict_same)
    params = _prep_params(L, np.asarray(pre_mix_g), np.asarray(post_mix_g), np.asarray(pre_ff_g), np.asarray(post_ff_g),
                          np.asarray(gmlp_ln_g), np.asarray(gmlp_ln_b), np.asarray(gmlp_ws), np.asarray(gmlp_bs),
                          np.asarray(gmlp_out_g), np.asarray(gla_gate_w), np.asarray(gla_gate_b), np.asarray(gla_out_g))
    big = dict(w_in=np.ascontiguousarray(w_in, dtype=np.float32), w_out=np.ascontiguousarray(w_out, dtype=np.float32),
               w_ff1=np.ascontiguousarray(w_ff1, dtype=np.float32), w_ff2=np.ascontiguousarray(w_ff2, dtype=np.float32))
    if n_cores == 8 and B == 4:
        owner = {0: 0, 1: 1, 4: 2, 5: 3}
    else:
        owner = {c: c for c in range(min(B, n_cores))}
    zeros = None
    in_maps = []
    for c in range(n_cores):
        if c in owner:
            m = dict(params)
            m.update(big)
            m["x"] = np.ascontiguousarray(x[owner[c]])
        else:
            if zeros is None:
                zeros = {k: np.zeros_like(v) for k, v in {**params, **big}.items()}
                zeros["x"] = np.zeros_like(x[0])
            m = dict(zeros)
        in_maps.append(m)
    res = run_bass_kernel_spmd(nc, in_maps, core_ids=list(range(n_cores)))
    inv = {b: c for c, b in owner.items()}
    out = np.stack([np.asarray(res.results[inv[b]]["y"], dtype=np.float32) for b in range(B)], axis=0)
    return out


def kernel(**inputs):
    return run_model(**inputs)
```

```python
from contextlib import ExitStack

import os
import numpy as np
import concourse.bass as bass
import concourse.mybir as mybir
from concourse.bass_utils import run_bass_kernel_spmd

AF = mybir.ActivationFunctionType
ALU = mybir.AluOpType
F32 = mybir.dt.float32
BF16 = mybir.dt.bfloat16

D = 1024
DIN = 2576
DFF = 4096
T = 512
NB = 4
EPS = 1e-6
STOP = int(os.environ.get('KSTOP', '99'))
DBG = os.environ.get('KDBG', '')
GS = int(os.environ.get('KGS', '99'))
NSLOT = 3


class Op:
    __slots__ = ("eng", "fn", "deps", "chan", "grp", "sigval", "sigchan", "waits", "is_dma")


class Rec:
    def __init__(self, strict_same=True):
        self.ops = []
        self.last_w = {}
        self.readers = {}
        self.strict_same = strict_same

    def add(self, eng, fn, r=(), w=(), chan=None, grp=None):
        i = len(self.ops)
        deps = {}
        for k in r:
            j = self.last_w.get(k)
            if j is not None:
                deps[j] = True
        for k in w:
            j = self.last_w.get(k)
            if j is not None and j not in deps:
                deps[j] = False
            for j in self.readers.get(k, {}).values():
                if j not in deps:
                    deps[j] = False
        for k in r:
            rk = ("dma", i) if chan is not None else eng
            self.readers.setdefault(k, {})[rk] = i
        for k in w:
            self.last_w[k] = i
            self.readers[k] = {}
        op = Op()
        op.eng = eng
        op.fn = fn
        op.chan = chan
        op.is_dma = chan is not None
        op.grp = grp
        op.deps = [(j, raw) for j, raw in deps.items() if j != i]
        op.sigval = None
        op.sigchan = None
        op.waits = []
        self.ops.append(op)
        return i

    def _needs_sync(self, pj, pi, raw):
        if pj.grp is not None and pj.grp == pi.grp:
            return False
        if pj.is_dma:
            return True
        if pj.eng != pi.eng:
            return True
        if pi.is_dma:
            return True
        if pj.eng == "tensor":
            return False
        return bool(raw and self.strict_same)

    def emit(self, nc, block, stack):
        ops = self.ops
        need_sig = [False] * len(ops)
        for op in ops:
            for j, raw in op.deps:
                if self._needs_sync(ops[j], op, raw):
                    need_sig[j] = True
                    op.waits.append(j)
        cnt = {}
        for i, op in enumerate(ops):
            if op.is_dma:
                cnt[op.chan] = cnt.get(op.chan, 0) + 16
                op.sigchan, op.sigval = op.chan, cnt[op.chan]
            elif need_sig[i]:
                cnt[op.eng] = cnt.get(op.eng, 0) + 1
                op.sigchan, op.sigval = op.eng, cnt[op.eng]
        sems = {}
        for ch in cnt:
            nm = ch if isinstance(ch, str) else "_".join(str(c) for c in ch)
            sems[ch] = stack.enter_context(nc.semaphore("s_" + nm))
        dma_final = {ch: v for ch, v in cnt.items() if ch not in ("tensor", "vector", "scalar", "gpsimd", "sync")}

        def run(eng_name):
            def body(e):
                waited = {}
                for op in ops:
                    if op.eng != eng_name:
                        continue
                    req = {}
                    for j in op.waits:
                        ch, v = ops[j].sigchan, ops[j].sigval
                        if v > req.get(ch, 0):
                            req[ch] = v
                    for ch, v in req.items():
                        if waited.get(ch, 0) >= v:
                            continue
                        e.wait_ge(sems[ch], v)
                        waited[ch] = v
                    ins = op.fn(e)
                    if op.sigval is not None:
                        ins.then_inc(sems[op.sigchan], 16 if op.is_dma else 1)
                if eng_name == "sync":
                    for ch, v in dma_final.items():
                        if waited.get(ch, 0) < v:
                            e.wait_ge(sems[ch], v)
            return body

        block.sync(run("sync"))
        block.gpsimd(run("gpsimd"))
        block.tensor(run("tensor"))
        block.vector(run("vector"))
        block.scalar(run("scalar"))


def build_program(n_tiles, n_layers, strict_same=True):
    L = n_layers
    S = n_tiles * T
    nc = bass.Bass("TRN2", target_bir_lowering=False)
    dt = nc.dram_tensor
    x_d = dt("x", [S, D], F32, kind="ExternalInput").ap()
    y_d = dt("y", [S, D], F32, kind="ExternalOutput").ap()
    win_d = dt("w_in", [L, D, DIN], F32, kind="ExternalInput").ap()
    wout_d = dt("w_out", [L, D, D], F32, kind="ExternalInput").ap()
    w1_d = dt("w_ff1", [L, D, DFF], F32, kind="ExternalInput").ap()
    w2_d = dt("w_ff2", [L, DFF, D], F32, kind="ExternalInput").ap()
    gv_d = dt("gv", [128, L * 4 * 8], F32, kind="ExternalInput").ap()
    hv_d = dt("hv", [128, L * 3 * 4], F32, kind="ExternalInput").ap()
    lnb_d = dt("lnb_bc", [128, L * 512], F32, kind="ExternalInput").ap()
    bs_d = dt("bs_row", [1, L * 512], F32, kind="ExternalInput").ap()
    wsT_d = dt("wsT", [128, L * 512], F32, kind="ExternalInput").ap()
    gwb_d = dt("gwb", [17, L * 256], F32, kind="ExternalInput").ap()
    cst_d = dt("consts", [128, 512], F32, kind="ExternalInput").ap()

    rec = Rec(strict_same=strict_same)
    stack = ExitStack()
    with stack:
        def sb(name, shape, dtype):
            return stack.enter_context(nc.sbuf_tensor(name, shape, dtype))

        def psum(name, shape, dtype):
            return stack.enter_context(nc.psum_tensor(name, shape, dtype))

        hT = sb("hT", [128, 8, T], F32)
        yT = sb("yT", [128, 8, T], BF16)
        sq = sb("sq", [128, 8, T], BF16)
        mT = sb("mT", [128, 8, T], F32)
        big = sb("big", [128, 32, T], BF16)
        wsl = [sb(f"wslot{k}", [128, 8192], BF16) for k in range(NSLOT)]
        rstd = sb("rstd", [128, T], F32)
        rstd2 = sb("rstd2", [128, T], F32)
        apre = sb("apre", [128, 4, T], F32)
        gpre = sb("gpre", [128, 4, T], F32)
        vg32 = [sb(f"vg32_{i}", [128, 512], F32) for i in range(2)]
        vc = sb("vc", [128, NB, 512], BF16)
        bnst4 = sb("bnst4", [128, NB, 8], F32)
        rstd4 = sb("rstd4", [128, NB], F32)
        mix = [sb("mix_0", [128, 512], F32)] * 2
        e1 = [sb("e1_0", [128, 256], F32)] * 2
        sp = [sb("sp_0", [128, 256], F32)] * 2
        cumT = [sb(f"cumT_{i}", [128, 256], F32) for i in range(2)]
        E1 = [sb(f"E1_{i}", [128, 256], F32) for i in range(2)]
        E2 = [sb("E2_0", [128, 256], F32)] * 2
        E3 = [sb("E3_0", [128, 256], F32)] * 2
        qeT = [sb(f"qeT_{i}", [128, 256], BF16) for i in range(2)]
        keT = [sb(f"keT_{i}", [128, 256], BF16) for i in range(2)]
        kdT = [sb(f"kdT_{i}", [128, 256], BF16) for i in range(2)]
        kd = [sb(f"kd_{i}", [128, 256], BF16) for i in range(2)]
        scm = [sb(f"scm_{i}", [128, 512], BF16) for i in range(2)]
        zlr = sb("zlr", [32, T], BF16)
        S32 = sb("S32", [128, L, 256], F32)
        Sb = sb("Sb", [128, L, 256], BF16)
        gv = sb("gv_s", [128, L * 32], F32)
        hv = sb("hv_s", [128, L * 12], F32)
        assert L <= 4
        mflat = mT[:].rearrange("p c t -> p (c t)")
        lnb = mflat[:, 0:L * 512]
        wm32 = mflat[:, 2048:2048 + L * 512]
        bsr = apre[:].rearrange("p c t -> p (c t)")[0:1, 0:L * 512]
        wmb = sb("wmb", [128, L * 512], BF16)
        Cc = sb("Cc", [128, L * 512], F32)
        gwb = sb("gwb_s", [17, L * 256], BF16)
        cst = sb("cst", [128, 512], F32)
        identb = sb("identb", [128, 128], BF16)
        triNb = sb("triNb", [128, 128], BF16)
        tri2b = sb("tri2b", [128, 256], BF16)
        sphi = sb("sphi", [128, 256], BF16)
        splo = sb("splo", [128, 256], BF16)
        onesD = sb("onesD", [128, 128], BF16)
        ones512 = sb("ones512", [128, 128], BF16)
        ones128 = sb("ones128", [128, 128], BF16)
        onerow = sb("onerow", [1, 128], F32)
        epsc = sb("epsc", [128, 1], F32)

        PS = [psum(f"ps{i}", [128, 512], F32) for i in range(7)]
        PSB = psum("psb", [128, 1024], BF16)

        ident32 = cst[:, 0:128]
        triN = cst[:, 128:256]
        tri01 = cst[:, 256:384]
        gmask = cst[:, 384:512]

        mmrot = [0]

        def mm_bank():
            b = mmrot[0] % 3
            mmrot[0] += 1
            return b

        def MM(out, lhsT, rhs, start, stop, r, w):
            rec.add("tensor", lambda e, o=out, l=lhsT, rr=rhs, s=start, p=stop: e.matmul(o, l, rr, start=s, stop=p), r=r, w=w)

        def TR(out, in_, ident, r, w):
            rec.add("tensor", lambda e, o=out, i=in_, d=ident: e.transpose(o, i, d), r=r, w=w)

        def ACT(out, in_, func, r, w, bias=None, scale=None):
            def fn(e, o=out, i=in_, f=func, b=bias, s=scale):
                kw = {}
                if b is not None:
                    kw["bias"] = b
                if s is not None:
                    kw["scale"] = s
                return e.activation(o, i, f, **kw)
            rec.add("scalar", fn, r=r, w=w)

        def V(fn, r, w):
            rec.add("vector", fn, r=r, w=w)

        def STT(out, in0, scalar, in1, op0, op1, r, w):
            V(lambda e, o=out, a=in0, s=scalar, b=in1, p0=op0, p1=op1: e.scalar_tensor_tensor(o, a, s, b, p0, p1), r, w)

        def TT(out, in0, in1, op, r, w):
            V(lambda e, o=out, a=in0, b=in1, p=op: e.tensor_tensor(o, a, b, p), r, w)

        def TS(out, in0, s1, s2, op0, op1, r, w):
            V(lambda e, o=out, a=in0, x1=s1, x2=s2, p0=op0, p1=op1: e.tensor_scalar(o, a, x1, x2, p0, p1), r, w)

        def VCOPY(out, in_, r, w):
            V(lambda e, o=out, i=in_: e.tensor_copy(o, i), r, w)

        def DMA(eng, out, in_, chan, r, w, grp=None):
            rec.add(eng, lambda e, o=out, i=in_: e.dma_start(out=o, in_=i), r=r, w=w, chan=chan, grp=grp)

        MK = [("mT", c) for c in range(8)]
        AK = [("apre", b) for b in range(NB)]
        DMA("sync", cst[:], cst_d[:, :], "ld0", [], ["cst"])
        DMA("sync", gv[:], gv_d[:, :], "ld1", [], ["gv"])
        DMA("sync", hv[:], hv_d[:, :], "ld2", [], ["hv"])
        DMA("sync", lnb, lnb_d[:, :], "ld3", [], MK)
        DMA("sync", bsr, bs_d[:, :], "ld4", [], AK)
        DMA("sync", wm32, wsT_d[:, :], "ld5", MK, MK)
        DMA("gpsimd", gwb[:], gwb_d[:, :], "ld6", [], ["gwb"])
        V(lambda e: e.memset(onesD[:], 1.0 / 1024.0), [], ["onesD"])
        V(lambda e: e.memset(ones512[:], 1.0 / 512.0), [], ["ones512"])
        V(lambda e: e.memset(ones128[:], 1.0 / 128.0), [], ["ones128"])
        V(lambda e: e.memset(onerow[:], 1.0), [], ["onerow"])
        V(lambda e: e.memset(epsc[:], EPS), [], ["epsc"])
        V(lambda e: e.memset(zlr[:], 1.0), [], ["zlr"])
        V(lambda e: e.memset(S32[:], 0.0), [], [("S32", l) for l in range(L)])
        V(lambda e: e.memset(Sb[:], 0.0), [], [("Sb", l) for l in range(L)])
        VCOPY(identb[:], ident32, ["cst"], ["identb"])
        VCOPY(triNb[:], triN, ["cst"], ["triNb"])
        VCOPY(tri2b[:, 0:128], tri01, ["cst"], ["tri2b"])
        VCOPY(tri2b[:, 128:256], tri01, ["cst", "tri2b"], ["tri2b"])
        for l in range(L):
            for h in range(4):
                sl = slice(l * 512 + h * 128, l * 512 + (h + 1) * 128)
                TT(wm32[:, sl], wm32[:, sl], gmask, ALU.mult, MK + ["cst"], MK)
        VCOPY(wmb[:], wm32, MK, ["wmb"])
        for l in range(L if 'nocmm' not in DBG else 0):
            bk = mm_bank()
            for h in range(4):
                sl = slice(l * 512 + h * 128, l * 512 + (h + 1) * 128)
                o = PS[bk][:, h * 128:(h + 1) * 128]
                if 'c_b' not in DBG:
                    MM(o, lnb[:, sl], wm32[:, sl], True, 'c_a' in DBG, MK, [("ps", bk)])
                if 'c_a' not in DBG:
                    MM(o, onerow[0:1, :], bsr[0:1, sl], 'c_b' in DBG, True, ["onerow"] + AK, [("ps", bk)])
            if 'c_nocopy' not in DBG:
                VCOPY(Cc[:, l * 512:(l + 1) * 512], PS[bk][:], [("ps", bk)], ["Cc"])

        slices = []
        for t in range(n_tiles):
            for l in range(L):
                slices += [(l, "in", 1), (l, "in", 2), (l, "in", 0), (l, "out", 0)]
                slices += [(l, "w1", i) for i in range(4)]
                slices += [(l, "w2", i) for i in range(4)]
        issued = [0]

        def issue_load(n):
            l, kind, i = slices[n]
            k = n % NSLOT
            slot = wsl[k]
            key = ("wslot", k)
            if kind == "in":
                c0 = i * 1024
                c1 = min(DIN, c0 + 1024)
                w = c1 - c0
                src = win_d[l].rearrange("(c p) n -> p c n", p=128)[:, :, c0:c1]
                dst = slot[:, 0:8 * w].rearrange("p (c n) -> p c n", c=8)
                DMA("gpsimd", dst, src, ("w", k), [], [key])
            elif kind == "out":
                src = wout_d[l].rearrange("(c p) n -> p c n", p=128)
                dst = slot[:, 0:8192].rearrange("p (c n) -> p c n", c=8)
                DMA("gpsimd", dst, src, ("w", k), [], [key])
            elif kind == "w1":
                src = w1_d[l].rearrange("(c p) n -> p c n", p=128)[:, :, i * 1024:(i + 1) * 1024]
                dst = slot[:, 0:8192].rearrange("p (c n) -> p c n", c=8)
                DMA("gpsimd", dst, src, ("w", k), [], [key])
            else:
                srcall = w2_d[l].rearrange("(c p) n -> p c n", p=128)[:, :, i * 256:(i + 1) * 256]
                dstall = slot[:, 0:8192].rearrange("p (c n) -> p c n", c=32)
                for q in range(4):
                    DMA("gpsimd", dstall[:, q * 8:(q + 1) * 8, :], srcall[:, q * 8:(q + 1) * 8, :], ("w", k), [], [key],
                        grp=("wfill", n))

        cur = [0]

        def next_slice(hold=0):
            n = cur[0]
            cur[0] += 1
            while issued[0] < min(len(slices), n + NSLOT - hold):
                issue_load(issued[0])
                issued[0] += 1
            return n % NSLOT

        def gcol(l, k, c):
            j = (l * 4 + k) * 8 + c
            return gv[:, j:j + 1]

        def hcol(l, k, h):
            j = (l * 3 + k) * 4 + h
            return hv[:, j:j + 1]

        def stats_rstd(sq_ap_fn, nchunk, ones_t, ones_key, sq_keys, out_rstd, out_key):
            for c in range(nchunk):
                MM(PS[3][:], ones_t[:], sq_ap_fn(c), c == 0, c == nchunk - 1, [sq_keys[c], ones_key], [("ps", 3)])
            if 'sqrt' in DBG:
                ACT(out_rstd[:], PS[3][:], AF.Sqrt, [("ps", 3), "epsc"], [out_key], bias=epsc[:, 0:1])
                V(lambda e, o=out_rstd: e.reciprocal(o[:], o[:]), [out_key], [out_key])
                return
            ACT(out_rstd[:], PS[3][:], AF.Ln, [("ps", 3), "epsc"], [out_key], bias=epsc[:, 0:1])
            ACT(out_rstd[:], out_rstd[:], AF.Exp, [out_key], [out_key], scale=-0.5)

        def pre_norm(l, k):
            for c in range(8):
                ACT(sq[:, c, :], hT[:, c, :], AF.Square, [("hT", c)], [("sq", c)])
            stats_rstd(lambda c: sq[:, c, :], 8, onesD, "onesD", [("sq", c) for c in range(8)], rstd, "rstd")
            for c in range(8):
                STT(yT[:, c, :], hT[:, c, :], gcol(l, k, c), rstd[:], ALU.mult, ALU.mult,
                    [("hT", c), "rstd", "gv"], [("yT", c)])

        def post_norm_residual(l, k):
            stats_rstd(lambda c: sq[:, c, :], 8, onesD, "onesD", [("sq", c) for c in range(8)], rstd, "rstd")
            def _scale(c):
                STT(mT[:, c, :], mT[:, c, :], gcol(l, k, c), rstd[:], ALU.mult, ALU.mult,
                    [("mT", c), "rstd", "gv"], [("mT", c)])

            def _add(c):
                TT(hT[:, c, :], hT[:, c, :], mT[:, c, :], ALU.add, [("hT", c), ("mT", c)], [("hT", c)])
            _scale(0)
            for c in range(1, 8):
                _scale(c)
                _add(c - 1)
            _add(7)

        def evac_m(bk, c):
            ACT(mT[:, c, :], PS[bk][:], AF.Copy, [("ps", bk)], [("mT", c)])
            ACT(sq[:, c, :], PS[bk][:], AF.Square, [("ps", bk)], [("sq", c)])

        U0, G0, CAT0, VG0, Q0, K0 = 0, 4, 8, 16, 20, 24

        qT = sb("qT", [128, 2, T], F32)
        kT = sb("kT", [128, 2, T], F32)

        def layer(t, l):
            pre_norm(l, 0)
            k = next_slice()
            W = wsl[k][:, 0:8192].rearrange("p (c n) -> p c n", c=8)
            wkey = ("wslot", k)
            for qc in range(2):
                bk = mm_bank()
                for kc in range(8):
                    MM(PS[bk][:], W[:, kc, qc * 128:(qc + 1) * 128], yT[:, kc, :], kc == 0, kc == 7,
                       [wkey, ("yT", kc)], [("ps", bk)])
                VCOPY(qT[:, qc, :], PS[bk][:], [("ps", bk)], [("qT", qc)])
            for qc in range(2):
                bk = mm_bank()
                for kc in range(8):
                    MM(PS[bk][:], W[:, kc, 256 + qc * 128:256 + (qc + 1) * 128], yT[:, kc, :], kc == 0, kc == 7,
                       [wkey, ("yT", kc)], [("ps", bk)])
                VCOPY(kT[:, qc, :], PS[bk][:], [("ps", bk)], [("kT", qc)])
            for b in range(NB):
                bk = mm_bank()
                for kc in range(8):
                    MM(PS[bk][:], yT[:, kc, b * 128:(b + 1) * 128], W[:, kc, 512:1024], kc == 0, kc == 7,
                       [wkey, ("yT", kc)], [("ps", bk)])
                VCOPY(big[:, VG0 + b, :], PS[bk][:], [("ps", bk)], [("big", VG0 + b)])
            kB = next_slice()
            WB = wsl[kB][:, 0:8 * 528].rearrange("p (c n) -> p c n", c=8)
            wkeyB = ("wslot", kB)
            bk = mm_bank()
            for kc in range(8):
                MM(PS[bk][0:16, :], WB[:, kc, 512:528], yT[:, kc, :], kc == 0, kc == 7,
                   [wkeyB, ("yT", kc)], [("ps", bk)])
            VCOPY(zlr[0:16, :], PS[bk][0:16, :], [("ps", bk)], ["zlr"])
            kC = next_slice(hold=1)
            WC = wsl[kC][:, 0:8192].rearrange("p (c n) -> p c n", c=8)
            wkeyC = ("wslot", kC)

            def f_g(gc):
                def fn():
                    bk = mm_bank()
                    for kc in range(8):
                        MM(PS[bk][:], WB[:, kc, gc * 128:(gc + 1) * 128], yT[:, kc, :], kc == 0, kc == 7,
                           [wkeyB, ("yT", kc)], [("ps", bk)])
                    ACT(big[:, G0 + gc, :], PS[bk][:], AF.Silu, [("ps", bk)], [("big", G0 + gc)])
                return fn

            def f_u(uc):
                def fn():
                    bk = mm_bank()
                    for kc in range(8):
                        MM(PS[bk][:], WC[:, kc, uc * 128:(uc + 1) * 128], yT[:, kc, :], kc == 0, kc == 7,
                           [wkeyC, ("yT", kc)], [("ps", bk)])
                    ACT(big[:, U0 + uc, :], PS[bk][:], AF.Gelu_apprx_tanh, [("ps", bk)], [("big", U0 + uc)])
                return fn

            def f_v(b):
                def fn():
                    i2 = b % 2
                    bk = mm_bank()
                    for kc in range(8):
                        MM(PS[bk][:], yT[:, kc, b * 128:(b + 1) * 128], WC[:, kc, 512:1024], kc == 0, kc == 7,
                           [wkeyC, ("yT", kc)], [("ps", bk)])
                    ACT(vg32[i2][:], PS[bk][:], AF.Gelu_apprx_tanh, [("ps", bk)], [("vg32", i2)])
                    V(lambda e, o=bnst4, i=vg32[i2]: e.bn_stats(o[:, b, 0:6], i[:]), [("vg32", i2)], [("bnst", b)])
                    V(lambda e, o=bnst4: e.bn_aggr(o[:, b, 6:8], o[:, b, 0:6]), [("bnst", b)], [("bnst", b)])
                    TS(vc[:, b, :], vg32[i2][:], bnst4[:, b, 6:7], None, ALU.subtract, ALU.bypass,
                       [("vg32", i2), ("bnst", b)], [("vc", b)])
                return fn

            def f_p(b):
                def fn():
                    i2 = 0
                    TS(vc[:, b, :], vc[:, b, :], rstd4[:, b:b + 1], None, ALU.mult, ALU.bypass,
                       [("vc", b), "rstd4"], [("vc", b)])
                    bk2 = mm_bank()
                    for h in range(4):
                        sl = slice(l * 512 + h * 128, l * 512 + (h + 1) * 128)
                        MM(PS[bk2][:, h * 128:(h + 1) * 128], vc[:, b, h * 128:(h + 1) * 128], wmb[:, sl], True, True,
                           [("vc", b), "wmb"], [("ps", bk2)])
                    for h in range(4):
                        sl = slice(l * 512 + h * 128, l * 512 + (h + 1) * 128)
                        STT(mix[i2][:, h * 128:(h + 1) * 128], PS[bk2][:, h * 128:(h + 1) * 128], hcol(l, 0, h), Cc[:, sl],
                            ALU.mult, ALU.add, [("ps", bk2), "hv", "Cc"], [("mix", 0)])
                    TT(gpre[:, :, b * 128:(b + 1) * 128], mix[i2][:].rearrange("p (h i) -> p h i", h=4),
                       big[:, U0:U0 + 4, b * 128:(b + 1) * 128], ALU.mult,
                       [("mix", 0)] + [("big", U0 + h) for h in range(4)], [("gpre", b)])
                return fn

            for uc in range(4):
                f_u(uc)()
            for b in range(NB):
                f_v(b)()
            for gc in range(4):
                f_g(gc)()
            ACT(rstd4[:], bnst4[:, :, 7], AF.Ln, [("bnst", b) for b in range(NB)] + ["epsc"], ["rstd4"], bias=epsc[:, 0:1])
            ACT(rstd4[:], rstd4[:], AF.Exp, ["rstd4"], ["rstd4"], scale=-0.5)
            fillers = [f_p(b) for b in range(NB)]

            def gla_block(b):
                i2 = b % 2
                bsl = slice(b * 128, (b + 1) * 128)
                MM(PS[4][:, 0:256], zlr[0:17, bsl], gwb[0:17, l * 256:(l + 1) * 256], True, True,
                   ["zlr", "gwb"], [("ps", 4)])
                ACT(e1[i2][:], PS[4][:, 0:256], AF.Exp, [("ps", 4)], [("e1", 0)], scale=-1.0)
                ACT(sp[i2][:], e1[i2][:], AF.Ln, [("e1", 0)], [("sp", 0)], bias=1.0)
                ACT(sphi[:], e1[i2][:], AF.Ln, [("e1", 0)], ["sphi"], bias=1.0)
                TT(splo[:], sp[i2][:], sphi[:], ALU.subtract, [("sp", 0), "sphi"], ["splo"])
                yield
                bkc = mm_bank()
                for dc in range(2):
                    o = PS[bkc][:, dc * 128:(dc + 1) * 128]
                    MM(o, sphi[:, dc * 128:(dc + 1) * 128], triNb[:], True, False, ["sphi", "triNb"], [("ps", bkc)])
                    MM(o, splo[:, dc * 128:(dc + 1) * 128], triNb[:], False, True, ["splo", "triNb"], [("ps", bkc)])
                ACT(cumT[i2][:], PS[bkc][:, 0:256], AF.Copy, [("ps", bkc)], [("cumT", i2)])
                ACT(E1[i2][:], cumT[i2][:], AF.Exp, [("cumT", i2)], [("E1", i2)])
                ACT(E2[i2][:], cumT[i2][:], AF.Exp, [("cumT", i2)], [("E2", 0)], scale=-1.0)
                for dc in range(2):
                    ACT(E3[i2][:, dc * 128:(dc + 1) * 128], cumT[i2][:, dc * 128:(dc + 1) * 128], AF.Exp,
                        [("cumT", i2)], [("E3", 0)], scale=-1.0, bias=cumT[i2][:, dc * 128 + 127:dc * 128 + 128])
                r3 = "p (c i) -> p c i"
                STT(qeT[i2][:].rearrange(r3, c=2), qT[:, :, bsl], 0.125, E1[i2][:].rearrange(r3, c=2), ALU.mult, ALU.mult,
                    [("qT", 0), ("qT", 1), ("E1", i2)], [("qeT", i2)])
                TT(keT[i2][:].rearrange(r3, c=2), kT[:, :, bsl], E2[i2][:].rearrange(r3, c=2), ALU.mult,
                   [("kT", 0), ("kT", 1), ("E2", 0)], [("keT", i2)])
                TT(kdT[i2][:].rearrange(r3, c=2), kT[:, :, bsl], E3[i2][:].rearrange(r3, c=2), ALU.mult,
                   [("kT", 0), ("kT", 1), ("E3", 0)], [("kdT", i2)])
                yield
                for dc in range(2):
                    TR(PSB[:, dc * 128:(dc + 1) * 128], kdT[i2][:, dc * 128:(dc + 1) * 128], identb[:],
                       [("kdT", i2), "identb"], [("psb", 0)])
                ACT(kd[i2][:], PSB[:, 0:256], AF.Copy, [("psb", 0)], [("kd", i2)])
                bks = (5, 4)
                for h in range(4):
                    dc, par = h // 2, h % 2
                    rows = slice(par * 64, par * 64 + 64)
                    MM(PS[bks[par]][:, dc * 128:(dc + 1) * 128], keT[i2][rows, dc * 128:(dc + 1) * 128],
                       qeT[i2][rows, dc * 128:(dc + 1) * 128], True, True,
                       [("keT", i2), ("qeT", i2)], [("ps", bks[par])])
                for par in range(2):
                    TT(scm[i2][:, par * 256:(par + 1) * 256], PS[bks[par]][:, 0:256], tri2b[:], ALU.mult,
                       [("ps", bks[par]), "tri2b"], [("scm", i2)])
                yield
                for h in range(4):
                    dc, par = h // 2, h % 2
                    rows = slice(par * 64, par * 64 + 64)
                    o = PS[6][:, h * 128:(h + 1) * 128]
                    MM(o, big[:, VG0 + b, h * 128:(h + 1) * 128], scm[i2][:, (par * 2 + dc) * 128:(par * 2 + dc + 1) * 128], True, False,
                       [("big", VG0 + b), ("scm", i2)], [("ps", 6)])
                    MM(o, Sb[rows, l, dc * 128:(dc + 1) * 128], qeT[i2][rows, dc * 128:(dc + 1) * 128], False, True,
                       [("Sb", l), ("qeT", i2)], [("ps", 6)])
                ACT(apre[:, :, bsl], PS[6][:].rearrange("p (h i) -> p h i", h=4), AF.Copy, [("ps", 6)], [("apre", b)])
                bk = mm_bank()
                for h in range(4):
                    dc = h // 2
                    MM(PS[bk][:, h * 128:(h + 1) * 128], kd[i2][:, dc * 128:(dc + 1) * 128],
                       big[:, VG0 + b, h * 128:(h + 1) * 128], True, True,
                       [("kd", i2), ("big", VG0 + b)], [("ps", bk)])
                for h in range(4):
                    dc, par = h // 2, h % 2
                    rows = slice(par * 64, par * 64 + 64)
                    STT(S32[rows, l, dc * 128:(dc + 1) * 128], S32[rows, l, dc * 128:(dc + 1) * 128],
                        E1[i2][rows, dc * 128 + 127:dc * 128 + 128], PS[bk][rows, h * 128:(h + 1) * 128],
                        ALU.mult, ALU.add, [("S32", l), ("E1", i2), ("ps", bk)], [("S32", l)])
                ACT(Sb[:, l, :], S32[:, l, :], AF.Copy, [("S32", l)], [("Sb", l)])
                yield

            fi = 0
            gens = [gla_block(b) for b in range(NB)]
            for kstep in range(NB + 3):
                for b in range(NB):
                    if 0 <= kstep - b < 4:
                        next(gens[b])
                        if fi < len(fillers):
                            fillers[fi]()
                            fi += 1
            while fi < len(fillers):
                fillers[fi]()
                fi += 1

            for h in range(4):
                ACT(sq[:, h, :], gpre[:, h, :], AF.Square, [("gpre", b) for b in range(NB)], [("sq", h)])
            for h in range(4):
                ACT(sq[:, 4 + h, :], apre[:, h, :], AF.Square, [("apre", b) for b in range(NB)], [("sq", 4 + h)])
            stats_rstd(lambda c: sq[:, c, :], 4, ones512, "ones512", [("sq", c) for c in range(4)], rstd, "rstd")
            for h in range(4):
                STT(big[:, CAT0 + h, :], gpre[:, h, :], hcol(l, 1, h), rstd[:], ALU.mult, ALU.mult,
                    [("gpre", b) for b in range(NB)] + ["rstd", "hv"], [("big", CAT0 + h)])
            for h in range(4):
                MM(PS[3][:], ones128[:], sq[:, 4 + h, :], True, True, [("sq", 4 + h), "ones128"], [("ps", 3)])
                ACT(rstd2[:], PS[3][:], AF.Ln, [("ps", 3), "epsc"], ["rstd2"], bias=epsc[:, 0:1])
                ACT(rstd2[:], rstd2[:], AF.Exp, ["rstd2"], ["rstd2"], scale=-0.5)
                STT(apre[:, h, :], apre[:, h, :], hcol(l, 2, h), rstd2[:], ALU.mult, ALU.mult,
                    [("apre", b) for b in range(NB)] + ["rstd2", "hv"], [("apre", b) for b in range(NB)])
                TT(big[:, CAT0 + 4 + h, :], apre[:, h, :], big[:, G0 + h, :], ALU.mult,
                   [("apre", b) for b in range(NB)] + [("big", G0 + h)], [("big", CAT0 + 4 + h)])

            k = next_slice()
            W = wsl[k][:, 0:8192].rearrange("p (c n) -> p c n", c=8)
            wkey = ("wslot", k)
            for dc in range(8):
                bk = mm_bank()
                for mc in range(8):
                    MM(PS[bk][:], W[:, mc, dc * 128:(dc + 1) * 128], big[:, CAT0 + mc, :], mc == 0, mc == 7,
                       [wkey, ("big", CAT0 + mc)], [("ps", bk)])
                evac_m(bk, dc)
            post_norm_residual(l, 1)

            if STOP <= 5:
                return
            pre_norm(l, 2)
            for s in range(4):
                k = next_slice()
                W = wsl[k][:, 0:8192].rearrange("p (c n) -> p c n", c=8)
                wkey = ("wslot", k)
                for fc in range(8):
                    ffc = s * 8 + fc
                    bk = mm_bank()
                    for kc in range(8):
                        MM(PS[bk][:], W[:, kc, fc * 128:(fc + 1) * 128], yT[:, kc, :], kc == 0, kc == 7,
                           [wkey, ("yT", kc)], [("ps", bk)])
                    r2 = ffc % 2
                    ACT(vg32[r2][:], PS[bk][:], AF.Relu, [("ps", bk)], [("vg32", r2)])
                    TT(big[:, ffc, :], vg32[r2][:], vg32[r2][:], ALU.mult, [("vg32", r2)], [("big", ffc)])
            for s in range(4):
                k = next_slice()
                W = wsl[k][:, 0:8192].rearrange("p (c n) -> p c n", c=32)
                wkey = ("wslot", k)
                for dl in range(2):
                    dc = s * 2 + dl
                    bk = mm_bank()
                    for ffc in range(32):
                        MM(PS[bk][:], W[:, ffc, dl * 128:(dl + 1) * 128], big[:, ffc, :], ffc == 0, ffc == 31,
                           [wkey, ("big", ffc)], [("ps", bk)])
                    evac_m(bk, dc)
            post_norm_residual(l, 3)

        stage = mflat.rearrange("p (b d) -> p b d", b=NB)
        for t in range(n_tiles):
            DMA("sync", stage, x_d[t * T:(t + 1) * T, :].rearrange("(b p) d -> p b d", p=128), "xin",
                [], [("mT", c) for c in range(8)])
            for c in range(8 if 'notr' not in DBG else 0):
                bk = mm_bank()
                for b in range(NB):
                    TR(PS[bk][:, b * 128:(b + 1) * 128], stage[:, b, c * 128:(c + 1) * 128], ident32,
                       [("mT", 2 * b), ("mT", 2 * b + 1), "cst"], [("ps", bk)])
                ACT(hT[:, c, :], PS[bk][:], AF.Copy, [("ps", bk)], [("hT", c)])
            for l in range(L):
                layer(t, l)
            for b in range(NB if 'notr' not in DBG else 0):
                for half in range(2):
                    bk = mm_bank()
                    for cc in range(4):
                        c = half * 4 + cc
                        TR(PS[bk][:, cc * 128:(cc + 1) * 128], hT[:, c, b * 128:(b + 1) * 128], ident32,
                           [("hT", c), "cst"], [("ps", bk)])
                    ACT(stage[:, b, half * 512:(half + 1) * 512], PS[bk][:], AF.Copy, [("ps", bk)], [("mT", 2 * b + half)])
            DMA("sync", y_d[t * T:(t + 1) * T, :].rearrange("(b p) d -> p b d", p=128), stage, "yout",
                [("mT", c) for c in range(8)], [("yout", t)])

        block = stack.enter_context(nc.Block())
        rec.emit(nc, block, stack)
    return nc


def _consts():
    c = np.zeros((128, 512), np.float32)
    j = np.arange(128)[:, None]
    i = np.arange(128)[None, :]
    c[:, 0:128] = np.eye(128, dtype=np.float32)
    c[:, 128:256] = np.where(j <= i, -1.0 / 16.0, 0.0)
    c[:, 256:384] = np.where(j <= i, 1.0, 0.0)
    c[:, 384:512] = np.where((i // 64) >= (j // 64), 1.0, 0.0)
    return c


def _prep_params(L, pre_mix_g, post_mix_g, pre_ff_g, post_ff_g, gmlp_ln_g, gmlp_ln_b, gmlp_ws, gmlp_bs,
                 gmlp_out_g, gla_gate_w, gla_gate_b, gla_out_g):
    f = np.float32
    gv = np.stack([pre_mix_g, post_mix_g, pre_ff_g, post_ff_g], axis=1).astype(f)
    gv = gv.reshape(L, 4, 8, 128).transpose(3, 0, 1, 2).reshape(128, L * 32)
    hv = np.stack([gmlp_ln_g, gmlp_out_g, gla_out_g], axis=1).astype(f)
    hv = hv.reshape(L, 3, 4, 128).transpose(3, 0, 1, 2).reshape(128, L * 12)
    lnb = np.broadcast_to(np.asarray(gmlp_ln_b, f).reshape(1, L * 512), (128, L * 512))
    bs = np.asarray(gmlp_bs, f).reshape(1, L * 512)
    wsT = np.asarray(gmlp_ws, f).transpose(3, 0, 1, 2).reshape(128, L * 512)
    gwb = np.concatenate([np.asarray(gla_gate_w, f), np.asarray(gla_gate_b, f)[:, None, :]], axis=1)
    gwb = gwb.transpose(1, 0, 2).reshape(17, L * 256)
    return dict(gv=np.ascontiguousarray(gv), hv=np.ascontiguousarray(hv), lnb_bc=np.ascontiguousarray(lnb),
                bs_row=np.ascontiguousarray(bs), wsT=np.ascontiguousarray(wsT), gwb=np.ascontiguousarray(gwb),
                consts=_consts())


def run_model(x, pre_mix_g, w_in, gmlp_ln_g, gmlp_ln_b, gmlp_ws, gmlp_bs, gmlp_out_g,
              gla_gate_w, gla_gate_b, gla_out_g, w_out, post_mix_g, pre_ff_g,
              w_ff1, w_ff2, post_ff_g, n_cores=8, strict_same=True):
    x = np.asarray(x, np.float32)
    B, S, _ = x.shape
    L = int(np.asarray(w_in).shape[0])
    n_tiles = S // T
    nc = build_program(n_tiles, L, strict_same=strict_same)
    params = _prep_params(L, np.asarray(pre_mix_g), np.asarray(post_mix_g), np.asarray(pre_ff_g), np.asarray(post_ff_g),
                          np.asarray(gmlp_ln_g), np.asarray(gmlp_ln_b), np.asarray(gmlp_ws), np.asarray(gmlp_bs),
                          np.asarray(gmlp_out_g), np.asarray(gla_gate_w), np.asarray(gla_gate_b), np.asarray(gla_out_g))
    big = dict(w_in=np.ascontiguousarray(w_in, dtype=np.float32), w_out=np.ascontiguousarray(w_out, dtype=np.float32),
               w_ff1=np.ascontiguousarray(w_ff1, dtype=np.float32), w_ff2=np.ascontiguousarray(w_ff2, dtype=np.float32))
    if n_cores == 8 and B == 4:
        owner = {0: 0, 1: 1, 4: 2, 5: 3}
    else:
        owner = {c: c for c in range(min(B, n_cores))}
    zeros = None
    in_maps = []
    for c in range(n_cores):
        if c in owner:
            m = dict(params)
            m.update(big)
            m["x"] = np.ascontiguousarray(x[owner[c]])
        else:
            if zeros is None:
                zeros = {k: np.zeros_like(v) for k, v in {**params, **big}.items()}
                zeros["x"] = np.zeros_like(x[0])
            m = dict(zeros)
        in_maps.append(m)
    res = run_bass_kernel_spmd(nc, in_maps, core_ids=list(range(n_cores)))
    inv = {b: c for c, b in owner.items()}
    out = np.stack([np.asarray(res.results[inv[b]]["y"], dtype=np.float32) for b in range(B)], axis=0)
    return out


def kernel(**inputs):
    return run_model(**inputs)
```
